# Optimizing a Trainium2 kernel written in Bass

```python
import math
import jax, jax.numpy as jnp
from jax import lax
import numpy as np

D_MODEL = 1024
BATCH = 16
SEQ = 2048
DEPTH = 1

D_MIX = 2 * D_MODEL
POOL_WINDOWS = (2, 4, 8, 16)
N_POOL_GROUPS = len(POOL_WINDOWS)
D_POOL = D_MIX // 4
POOL_GROUP = D_POOL // N_POOL_GROUPS
D_SSD = D_MIX - D_POOL
SSD_HEAD_DIM = 64
N_SSD_HEADS = D_SSD // SSD_HEAD_DIM
N_BC_GROUPS = 4
HEADS_PER_GROUP = N_SSD_HEADS // N_BC_GROUPS
D_STATE = 128
CONV_WIDTH = 4
CHUNK = 128
D_CONV = D_SSD + 2 * N_BC_GROUPS * D_STATE
D_IN_PROJ = D_POOL + D_SSD + D_CONV + N_SSD_HEADS
D_FF = 4 * D_MODEL
ALPHA = (2.0 * DEPTH) ** 0.25
BETA = (8.0 * DEPTH) ** -0.25
LN_EPS = 1e-5
RMS_EPS = 1e-5

kernel_name = "hybrid_pool_ssd_deepnorm_block"


def layer_norm(x, g, b):
    xf = x.astype(jnp.float32)
    mu = jnp.mean(xf, axis=-1, keepdims=True)
    var = jnp.mean(jnp.square(xf - mu), axis=-1, keepdims=True)
    return ((xf - mu) * lax.rsqrt(var + LN_EPS) * g + b).astype(x.dtype)


def pool_mixer(u, w_pool, pool_scale):
    b, s, _ = u.shape
    uf = u.astype(jnp.float32).reshape(b, s, N_POOL_GROUPS, POOL_GROUP)
    cs = jnp.pad(jnp.cumsum(uf, axis=1), ((0, 0), (1, 0), (0, 0), (0, 0)))
    pos = jnp.arange(1, s + 1)
    pooled = []
    for gi, w in enumerate(POOL_WINDOWS):
        c = cs[:, :, gi]
        lag = jnp.pad(c, ((0, 0), (w, 0), (0, 0)))[:, : s + 1]
        win_sum = (c - lag)[:, 1:]
        cnt = jnp.minimum(pos, w).astype(jnp.float32)
        pooled.append(win_sum / cnt[None, :, None])
    pooled = jnp.stack(pooled, axis=2)
    diff = (pooled - uf).astype(u.dtype)
    y = jnp.einsum("bsgc,gcd->bsgd", diff, w_pool)
    return y.reshape(b, s, D_POOL) * pool_scale


def causal_depthwise_conv(u, w, bias):
    ch = u.shape[-1]
    y = lax.conv_general_dilated(
        u, w[:, None, :].astype(u.dtype), window_strides=(1,),
        padding=[(CONV_WIDTH - 1, 0)], dimension_numbers=("NWC", "WIO", "NWC"),
        feature_group_count=ch)
    return y + bias


def ssd_chunked(x, dt, a_neg, bm, cm):
    b, s, h, p = x.shape
    nc = s // CHUNK
    g, r = N_BC_GROUPS, HEADS_PER_GROUP
    xdt = (x.astype(jnp.float32) * dt[..., None]).reshape(b, nc, CHUNK, g, r, p)
    bc = bm.astype(jnp.float32).reshape(b, nc, CHUNK, g, D_STATE)
    cc = cm.astype(jnp.float32).reshape(b, nc, CHUNK, g, D_STATE)
    a = (dt * a_neg).reshape(b, nc, CHUNK, g, r).transpose(0, 3, 4, 1, 2)
    a_cs = jnp.cumsum(a, axis=-1)
    causal = jnp.tril(jnp.ones((CHUNK, CHUNK), dtype=bool))
    seg = a_cs[..., :, None] - a_cs[..., None, :]
    decay = jnp.exp(jnp.where(causal, seg, -jnp.inf))
    cb = jnp.einsum("bclgn,bcsgn->bcgls", cc, bc)
    y_diag = jnp.einsum("bcgls,bgrcls,bcsgrp->bclgrp", cb, decay, xdt)
    decay_to_end = jnp.exp(a_cs[..., -1:] - a_cs)
    states = jnp.einsum("bclgn,bgrcl,bclgrp->bcgrpn", bc, decay_to_end, xdt)
    chunk_decay = jnp.exp(a_cs[..., -1])

    def step(carry, inp):
        st, dec = inp
        return carry * dec[..., None, None] + st, carry

    h0 = jnp.zeros((b, g, r, p, D_STATE), jnp.float32)
    _, prev = lax.scan(step, h0, (jnp.moveaxis(states, 1, 0), jnp.moveaxis(chunk_decay, -1, 0)))
    y_off = jnp.einsum("bclgn,cbgrpn,bgrcl->bclgrp", cc, prev, jnp.exp(a_cs))
    return (y_diag + y_off).reshape(b, s, h, p)


def gated_rmsnorm(y, z, w):
    v = y.astype(jnp.float32) * jax.nn.silu(z.astype(jnp.float32))
    lead = v.shape[:-1]
    vg = v.reshape(*lead, N_BC_GROUPS, D_SSD // N_BC_GROUPS)
    vg = vg * lax.rsqrt(jnp.mean(jnp.square(vg), axis=-1, keepdims=True) + RMS_EPS)
    return vg.reshape(*lead, D_SSD) * w


def mixer_sublayer(x, w_in, w_pool, pool_scale, conv_w, conv_b, dt_bias, a_log,
                   d_skip, ssd_norm_w, w_out):
    b, s, _ = x.shape
    proj = x @ w_in
    u_pool, z, xbc, dt_raw = jnp.split(
        proj, [D_POOL, D_POOL + D_SSD, D_POOL + D_SSD + D_CONV], axis=-1)
    y_pool = pool_mixer(u_pool, w_pool, pool_scale)
    xbc = jax.nn.silu(causal_depthwise_conv(xbc, conv_w, conv_b))
    xs, bm, cm = jnp.split(xbc, [D_SSD, D_SSD + N_BC_GROUPS * D_STATE], axis=-1)
    xs = xs.reshape(b, s, N_SSD_HEADS, SSD_HEAD_DIM)
    bm = bm.reshape(b, s, N_BC_GROUPS, D_STATE)
    cm = cm.reshape(b, s, N_BC_GROUPS, D_STATE)
    dt = jax.nn.softplus(dt_raw.astype(jnp.float32) + dt_bias)
    a_neg = -jnp.exp(a_log.astype(jnp.float32))
    y = ssd_chunked(xs, dt, a_neg, bm, cm) + d_skip[:, None] * xs
    y_ssd = gated_rmsnorm(y.reshape(b, s, D_SSD), z, ssd_norm_w)
    mixed = jnp.concatenate([y_pool.astype(x.dtype), y_ssd.astype(x.dtype)], axis=-1)
    return mixed @ w_out


def setup_inputs(seed: int = 0) -> dict:
    key = jax.random.key(seed)
    ks = jax.random.split(key, 17)
    f32 = jnp.float32
    L = DEPTH
    nrm = lambda k, shp: jax.random.normal(k, shp, f32)
    x = nrm(ks[0], (BATCH, SEQ, D_MODEL))
    w_in = nrm(ks[1], (L, D_MODEL, D_IN_PROJ)) * D_MODEL ** -0.5
    w_pool = nrm(ks[2], (L, N_POOL_GROUPS, POOL_GROUP, POOL_GROUP)) * POOL_GROUP ** -0.5
    pool_scale = 1.0 + 0.02 * nrm(ks[3], (L, D_POOL))
    bound = CONV_WIDTH ** -0.5
    conv_w = jax.random.uniform(ks[4], (L, CONV_WIDTH, D_CONV), f32, -bound, bound)
    conv_b = jax.random.uniform(ks[5], (L, D_CONV), f32, -bound, bound)
    dt0 = jnp.exp(jax.random.uniform(ks[6], (L, N_SSD_HEADS), f32, math.log(1e-3), math.log(1e-1)))
    dt_bias = dt0 + jnp.log(-jnp.expm1(-dt0))
    a_log = jnp.log(jax.random.uniform(ks[7], (L, N_SSD_HEADS), f32, 1.0, 16.0))
    d_skip = 1.0 + 0.1 * nrm(ks[8], (L, N_SSD_HEADS))
    ssd_norm_w = 1.0 + 0.02 * nrm(ks[9], (L, D_SSD))
    w_out = nrm(ks[10], (L, D_MIX, D_MODEL)) * (D_MIX ** -0.5) * BETA
    ln1_g = 1.0 + 0.02 * nrm(ks[11], (L, D_MODEL))
    ln1_b = 0.02 * nrm(ks[12], (L, D_MODEL))
    w_ff1 = nrm(ks[13], (L, D_MODEL, D_FF)) * D_MODEL ** -0.5
    w_ff2 = nrm(ks[14], (L, D_FF, D_MODEL)) * (D_FF ** -0.5) * BETA
    ln2_g = 1.0 + 0.02 * nrm(ks[15], (L, D_MODEL))
    ln2_b = 0.02 * nrm(ks[16], (L, D_MODEL))
    return {"x": x, "w_in": w_in, "w_pool": w_pool, "pool_scale": pool_scale,
            "conv_w": conv_w, "conv_b": conv_b, "dt_bias": dt_bias, "a_log": a_log,
            "d_skip": d_skip, "ssd_norm_w": ssd_norm_w, "w_out": w_out,
            "ln1_g": ln1_g, "ln1_b": ln1_b, "w_ff1": w_ff1, "w_ff2": w_ff2,
            "ln2_g": ln2_g, "ln2_b": ln2_b}


def reference(x, w_in, w_pool, pool_scale, conv_w, conv_b, dt_bias, a_log, d_skip,
              ssd_norm_w, w_out, ln1_g, ln1_b, w_ff1, w_ff2, ln2_g, ln2_b):
    h = x
    for l in range(DEPTH):
        mix = mixer_sublayer(h, w_in[l], w_pool[l], pool_scale[l], conv_w[l], conv_b[l],
                             dt_bias[l], a_log[l], d_skip[l], ssd_norm_w[l], w_out[l])
        h = layer_norm(ALPHA * h + mix.astype(h.dtype), ln1_g[l], ln1_b[l])
        ff = jnp.square(jax.nn.relu(h @ w_ff1[l])) @ w_ff2[l]
        h = layer_norm(ALPHA * h + ff.astype(h.dtype), ln2_g[l], ln2_b[l])
    return h
```

```python
import math
from contextlib import ExitStack
import numpy as np
import concourse.bass as bass
import concourse.mybir as mybir
from concourse.bass_utils import run_bass_kernel_spmd

F32 = mybir.dt.float32
BF16 = mybir.dt.bfloat16
F32R = mybir.dt.float32r
AF = mybir.ActivationFunctionType
ALU = mybir.AluOpType

NCORES = 8
D = 1024
TOK = 4096
TT = 512
NT = TOK // TT
SEQ = 2048
DIN = 4632
DFF = 4096
ALPHA = 2.0 ** 0.25
LN_EPS = 1e-5
RMS_EPS = 1e-5
WINS = (2, 4, 8, 16)

ENGS = ("pe", "act", "dve", "pool", "sp")


class Prog:
    def __init__(self, nc):
        self.nc = nc
        self.ops = []
        self.lastw = {}
        self.readers = {}
        self.streams = {}
        self.last_eng = {}
        self.last_stream = {}
        self.bar = None

    def _ekey(self, i):
        o = self.ops[i]
        return ("s", o["stream"]) if o["stream"] is not None else ("e", o["eng"])

    def add(self, eng, fn, reads=(), writes=(), stream=None, extra=(), nobar=False):
        i = len(self.ops)
        deps = set(extra)
        raw = set()
        for k in reads:
            w = self.lastw.get(k)
            if w is not None:
                deps.add(w)
                raw.add(w)
        for k in writes:
            w = self.lastw.get(k)
            if w is not None:
                deps.add(w)
            deps.update(self.readers.get(k, {}).values())
        if self.bar is not None and not nobar:
            deps.add(self.bar)
        fdeps = set()
        for d in deps:
            od = self.ops[d]
            if od["stream"] is None and stream is None and od["eng"] == eng:
                if eng == "pe" or d not in raw:
                    continue
            fdeps.add(d)
        sval = None
        if stream is not None:
            self.streams[stream] = self.streams.get(stream, 0) + 1
            sval = 16 * self.streams[stream]
        self.ops.append(dict(eng=eng, fn=fn, deps=fdeps, stream=stream, sval=sval, hasdep=False))
        for d in fdeps:
            self.ops[d]["hasdep"] = True
        ek = self._ekey(i)
        for k in reads:
            self.readers.setdefault(k, {})[ek] = i
        for k in writes:
            self.lastw[k] = i
            self.readers[k] = {}
        if stream is None:
            self.last_eng[eng] = i
        else:
            self.last_stream[stream] = i
        return i

    def barrier(self, fn):
        front = set(self.last_eng.values()) | set(
            v for k, v in self.last_stream.items() if not (k.startswith("wcast_") or k.startswith("ring")))
        self.bar = None
        b = self.add("pool", fn, extra=front)
        self.bar = b
        keep = lambda k: k.startswith("ring") or k.startswith("wq_") or k.startswith("x_tok")
        self.lastw = {k: v for k, v in self.lastw.items() if keep(k)}
        self.readers = {k: v for k, v in self.readers.items() if keep(k)}

    def emit(self, ctx):
        nc = self.nc
        ops = self.ops
        esem = {e: ctx.enter_context(nc.semaphore("e_" + e)) for e in ENGS}
        ssem = {s: ctx.enter_context(nc.semaphore("s_" + s)) for s in self.streams}
        cnt = {e: 0 for e in ENGS}
        for o in ops:
            if o["stream"] is None:
                if o["hasdep"]:
                    cnt[o["eng"]] += 1
                    o["tick"] = cnt[o["eng"]]
                else:
                    o["tick"] = None
        engvc = {e: {} for e in ENGS}
        for o in ops:
            vc = engvc[o["eng"]]
            wm = {}
            for d in sorted(o["deps"]):
                od = ops[d]
                if od["stream"] is not None:
                    key, val = ("s", od["stream"]), od["sval"]
                else:
                    key, val = ("e", od["eng"]), od["tick"]
                if vc.get(key, 0) < val:
                    wm[key] = max(wm.get(key, 0), val)
                    vc[key] = val
                for k2, v2 in od["vc"].items():
                    if vc.get(k2, 0) < v2:
                        vc[k2] = v2
            o["waits"] = wm
            snap = dict(vc)
            if o["stream"] is None and o["tick"] is not None:
                snap[("e", o["eng"])] = o["tick"]
            o["vc"] = snap
        final = [(("s", s), 16 * c) for s, c in self.streams.items()]
        self.stats = dict(nops=len(ops), nwaits=sum(len(o["waits"]) for o in ops), ticks=dict(cnt),
                          per_eng={e: sum(1 for o in ops if o["eng"] == e) for e in ENGS})

        def sem_of(key):
            return ssem[key[1]] if key[0] == "s" else esem[key[1]]

        def run(engname):
            def body(eng):
                for o in ops:
                    if o["eng"] != engname:
                        continue
                    for k, v in o["waits"].items():
                        eng.wait_ge(sem_of(k), v)
                    ins = o["fn"](eng)
                    if o["stream"] is not None:
                        ins.then_inc(ssem[o["stream"]], 16)
                    elif o["tick"] is not None:
                        ins.then_inc(esem[engname], 1)
                if engname == "sp":
                    for k, v in final:
                        eng.wait_ge(sem_of(k), v)
            return body

        with nc.Block() as block:
            block.tensor(run("pe"))
            block.scalar(run("act"))
            block.vector(run("dve"))
            block.gpsimd(run("pool"))
            block.sync(run("sp"))


DEBUG = False


def build_nc():
    nc = bass.Bass("TRN2", target_bir_lowering=False)
    dbg = {}

    def dbg_out(name, shape):
        dbg[name] = nc.dram_tensor(name, list(shape), F32, kind="ExternalOutput").ap()
        return dbg[name]

    def din(name, shape):
        return nc.dram_tensor(name, list(shape), F32, kind="ExternalInput").ap()

    x_d = din("x", [TOK, D])
    w_in_d = din("w_in", [D, DIN])
    w_out_d = din("w_out", [2048, D])
    w_ff1_d = din("w_ff1", [D, DFF])
    w_ff2_d = din("w_ff2", [DFF, D])
    wpool_d = din("w_pool_l", [128, 4, 128])
    convw_d = din("conv_w_l", [128, 80])
    convb_d = din("conv_b_l", [128, 20])
    poolsc_d = din("pool_scale_l", [128, 4])
    normw_d = din("ssd_norm_w_l", [128, 12])
    ln1g_d = din("ln1_g_l", [128, 8])
    ln1b_d = din("ln1_b_l", [128, 8])
    ln2g_d = din("ln2_g_l", [128, 8])
    ln2b_d = din("ln2_b_l", [128, 8])
    dtb_d = din("dt_bias_r", [128, 24])
    alog_d = din("a_log_r", [128, 24])
    dsk_d = din("d_skip_r", [128, 24])
    out_d = nc.dram_tensor("out", [TOK, D], F32, kind="ExternalOutput").ap()
    wq_in = nc.dram_tensor("wq_in", [D, DIN], BF16, kind="Internal").ap()
    wq_out = nc.dram_tensor("wq_out", [2048, D], BF16, kind="Internal").ap()
    wq_ff1 = nc.dram_tensor("wq_ff1", [D, DFF], BF16, kind="Internal").ap()
    wq_ff2 = nc.dram_tensor("wq_ff2", [DFF, D], BF16, kind="Internal").ap()

    with ExitStack() as ctx:
        TOTAL = 206848
        big = ctx.enter_context(nc.sbuf_tensor("big", [128, TOTAL // 2], BF16))
        pst = [ctx.enter_context(nc.psum_tensor("ps%d" % i, [128, 512], F32)) for i in range(8)]
        state = dict(off=0)

        def carve_at(off, free, dt):
            nb = int(np.prod(free)) * (2 if dt == BF16 else 4)
            assert off % 4 == 0 and off + nb <= TOTAL, (off, nb)
            v = big[:, off // 2:(off + nb) // 2]
            if dt != BF16:
                v = v.bitcast(dt)
            if len(free) == 2:
                v = v.rearrange("p (a b) -> p a b", a=free[0])
            return v

        def carve(free, dt):
            nb = int(np.prod(free)) * (2 if dt == BF16 else 4)
            nb = (nb + 63) // 64 * 64
            v = carve_at(state["off"], free, dt)
            state["off"] += nb
            return v

        convdiag = carve([80, 128], BF16)
        NRING = 6
        ring_off = state["off"]
        ring = [carve([8, 512], BF16) for _ in range(NRING)]
        bgf = carve_at(ring_off + 3 * 8192, [8, 512], F32)
        bgb = ring[5]
        x_tok = carve([4, 1024], F32)
        resid = carve([8, 512], F32)
        ident_b = carve([128], BF16)
        ident_f = carve([128], F32)
        U_f = carve([128], F32)
        ones_f = carve([128], F32)
        SL_f = carve([128], F32)
        SL_r = ctx.enter_context(nc.sbuf_tensor("SL_r", [128, 128], F32R))[:]
        U_b = carve([128], BF16)
        ones_b = carve([128], BF16)
        negh = carve([512], F32)
        epsln = carve([16], F32)
        convw = carve([80], F32)
        convb = carve([20], F32)
        poolsc = carve([4], F32)
        normw = carve([12], F32)
        ln1g = carve([8], F32)
        ln1b = carve([8], F32)
        ln2g = carve([8], F32)
        ln2b = carve([8], F32)
        dtb = carve([24], F32)
        aneg = carve([24], F32)
        dsk = carve([24], F32)
        wpool_b = carve([4, 128], BF16)
        xbchalo = carve([20, 3], BF16)
        poolhalo = carve([4, 15], F32)
        prevf = carve([4, 384], F32)
        prevb = carve([4, 384], BF16)
        bart = carve([16], F32)
        S = state["off"]
        SCR = TOTAL - S
        assert SCR >= 81920, SCR

        xbcT = carve_at(S + 0, [20, 512], BF16)
        sz = carve_at(S + 20480, [4, 1536], BF16)
        mixedT = carve_at(S + 32768, [16, 512], BF16)
        dt_t = carve_at(S + 49152, [4, 24], F32)
        a_t = carve_at(S + 49152 + 384, [4, 24], F32)
        T0 = S + 49920
        xTb = carve_at(T0, [8, 512], BF16)
        o1 = T0 + 8192
        poolA = carve_at(o1, [4, 527], F32); o1 += 8448
        poolB = carve_at(o1, [527], F32); o1 += 2112
        poolC = carve_at(o1, [527], F32); o1 += 2112
        diff = carve_at(o1, [4, 512], BF16); o1 += 4096
        ext = [carve_at(o1, [516], BF16), carve_at(o1 + 1040, [516], BF16)]; o1 += 2080
        dtmp = carve_at(o1, [4, 24], F32); o1 += 384
        dexp = carve_at(o1, [4, 24], F32); o1 += 384
        x_bf = carve_at(o1, [4, 1024], BF16); o1 += 8192
        assert o1 <= TOTAL, (o1, TOTAL)
        o2 = T0
        xdt = carve_at(o2, [24, 64], BF16); o2 += 3072
        xsD = carve_at(o2, [24, 64], BF16); o2 += 3072
        xdec = carve_at(o2, [24, 64], BF16); o2 += 3072
        Btok = carve_at(o2, [4, 128], BF16); o2 += 1024
        CBm2 = [carve_at(o2, [4, 128], BF16), carve_at(o2 + 1024, [4, 128], BF16)]; o2 += 2048
        R3 = [ctx.enter_context(nc.sbuf_tensor("R3_%d" % i, [128, 384], F32R))[:].rearrange("p (h l) -> p h l", h=3)
              for i in range(2)]
        L3 = [carve_at(o2, [3, 128], BF16), carve_at(o2 + 768, [3, 128], BF16)]; o2 += 1536
        M3 = [carve_at(o2 + 768 * m, [3, 128], BF16) for m in range(3)]; o2 += 2304
        t1 = [carve_at(o2, [6, 64], F32), carve_at(o2 + 1536, [6, 64], F32)]; o2 += 3072
        t2 = [carve_at(o2, [6, 64], F32), carve_at(o2 + 1536, [6, 64], F32)]; o2 += 3072
        vv = carve_at(o2, [4, 384], F32); o2 += 6144
        vn = carve_at(o2, [4, 384], BF16); o2 += 3072
        junk = carve_at(o2, [384], F32); o2 += 1536
        acs = carve_at(o2, [24], F32); o2 += 128
        ea = carve_at(o2, [24], F32); o2 += 128
        cd = carve_at(o2, [24], F32); o2 += 128
        dte = carve_at(o2, [24], F32); o2 += 128
        tmpd = carve_at(o2, [24], F32); o2 += 128
        ss = carve_at(o2, [4], F32); o2 += 64
        ms = carve_at(o2, [4], F32); o2 += 64
        rstd4 = carve_at(o2, [4], F32); o2 += 64
        lnms = carve_at(o2, [4], F32); o2 += 64
        assert o2 <= TOTAL
        hTb = carve_at(S + 0, [8, 512], BF16)
        hid = carve_at(S + 8192, [32, 512], BF16)
        rl = [carve_at(S + 40960, [512], BF16), carve_at(S + 41984, [512], BF16)]
        ostage4 = [carve_at(S + 43008 + 4096 * i, [1024], F32) for i in range(4)]
        LN3 = T0
        LN4 = S + 59392
        assert LN4 + 22528 <= TOTAL and LN3 + 22528 <= TOTAL

        P = Prog(nc)
        bank_ctr = [0]

        def dump(name, ap, shape):
            if not DEBUG:
                return
            d = dbg_out(name, shape)
            front = set(P.last_eng.values()) | set(v for k, v in P.last_stream.items() if not k.startswith("ring"))
            i = P.add("pool", lambda e, d=d, ap=ap: e.dma_start(out=d, in_=ap), stream="dbg", extra=front)
            P.add("pool", lambda e: e.memset(bart, 0.0), extra=[i])
            P.barrier(lambda e: e.memset(bart, 0.0))

        bank_pool = [list(range(8))]

        def next_bank():
            pool_ = bank_pool[0]
            b = pool_[bank_ctr[0] % len(pool_)]
            bank_ctr[0] += 1
            return b

        def psk(b):
            return "ps%d" % b

        ring_ctr = [0]
        ring_n = [3]

        def ring_load(wq, wkey, kq, c0, n):
            s = ring_ctr[0] % ring_n[0]
            ring_ctr[0] += 1
            wkey = "%s_b%d_%d" % (wkey, kq, c0 // 512)

            src = wq.rearrange("(kc p) n -> p kc n", p=128)[:, kq * 8:(kq + 1) * 8, c0:c0 + n]
            P.add("sp", lambda e, s=s, src=src, n=n: e.dma_start(out=ring[s][:, :, 0:n], in_=src),
                  reads=[wkey], writes=["ring%d" % s], stream="ring%d" % s, nobar=True)
            return s

        def gemm(wq, wkey, K, c0, ncols, rhsT, rkeys, evac, kc_outer_first=False):
            nkq = K // 1024
            for nb in range((ncols + 511) // 512):
                n = min(512, ncols - nb * 512)
                nch = n // 128
                banks = [next_bank() for _ in range(nch)]
                for kq in range(nkq):
                    s = ring_load(wq, wkey, kq, c0 + nb * 512, n)
                    order = [(fcl, kc) for fcl in range(nch) for kc in range(8)]
                    if kc_outer_first and nb == 0:
                        order = [(fcl, kc) for kc in range(8) for fcl in range(nch)]
                    for fcl, kc in order:
                        if True:
                            P.add("pe", lambda e, b=banks[fcl], s=s, kc=kc, fcl=fcl, kq=kq:
                                  e.matmul(pst[b][:], lhsT=ring[s][:, kc, fcl * 128:(fcl + 1) * 128],
                                           rhs=rhsT[:, kq * 8 + kc, :],
                                           start=(kq == 0 and kc == 0), stop=(kq == nkq - 1 and kc == 7)),
                                  reads=["ring%d" % s, rkeys(kq * 8 + kc)], writes=[psk(banks[fcl])])
                for fcl in range(nch):
                    evac(nb * 4 + fcl, banks[fcl])

        def cast_w(src, dst, rows, key, extra=()):
            for r in range(rows // 128):
                P.add("pool", lambda e, r=r: e.dma_start(out=dst[r * 128:(r + 1) * 128, :],
                                                         in_=src[r * 128:(r + 1) * 128, :]),
                      writes=[key], stream="wcast_" + key, nobar=True, extra=extra)
        stg_f = [carve_at(S + i * 16384, [8, 512], F32) for i in range(3)]
        stg_b = [carve_at(S + 49152 + i * 8192, [8, 512], BF16) for i in range(3)]
        sctr = [0]

        def stage_cast(src, dst, K, ncols, kname):
            for kq in range(K // 1024):
                for blk in range((ncols + 511) // 512):
                    n = min(512, ncols - blk * 512)
                    i = sctr[0] % 3
                    sctr[0] += 1
                    sv = src[kq * 1024:(kq + 1) * 1024, :].rearrange("(kc p) n -> p kc n", p=128)[
                        :, :, blk * 512:blk * 512 + n]
                    dv = dst[kq * 1024:(kq + 1) * 1024, :].rearrange("(kc p) n -> p kc n", p=128)[
                        :, :, blk * 512:blk * 512 + n]
                    P.add("sp", lambda e, i=i, sv=sv, n=n: e.dma_start(out=stg_f[i][:, :, 0:n], in_=sv),
                          writes=["stgf%d" % i], stream="stgl%d" % i)
                    ce = ("dve", "act")[sctr[0] % 2]
                    if ce == "dve":
                        P.add("dve", lambda e, i=i, n=n: e.tensor_copy(out=stg_b[i][:, :, 0:n],
                                                                       in_=stg_f[i][:, :, 0:n]),
                              reads=["stgf%d" % i], writes=["stgb%d" % i])
                    else:
                        P.add("act", lambda e, i=i, n=n: e.activation(out=stg_b[i][:, :, 0:n],
                                                                      in_=stg_f[i][:, :, 0:n], func=AF.Copy),
                              reads=["stgf%d" % i], writes=["stgb%d" % i])
                    P.add("act", lambda e, i=i, dv=dv, n=n: e.dma_start(out=dv, in_=stg_b[i][:, :, 0:n]),
                          reads=["stgb%d" % i], writes=["%s_b%d_%d" % (kname, kq, blk)], stream="stgs%d" % i)
        stage_cast(w_in_d, wq_in, D, DIN, "wq_in")
        stage_cast(w_out_d, wq_out, 2048, D, "wq_out")
        stage_cast(w_ff1_d, wq_ff1, D, DFF, "wq_ff1")
        bg_list = [(w_ff2_d, wq_ff2, "wq_ff2", kq, blk) for blk in range(2) for kq in range(4)]
        bg_pos = [0]

        def bg_views(k):
            src, dst, kname, kq, blk = bg_list[k]
            sv = src[kq * 1024:(kq + 1) * 1024, :].rearrange("(kc p) n -> p kc n", p=128)[:, :, blk * 512:(blk + 1) * 512]
            dv = dst[kq * 1024:(kq + 1) * 1024, :].rearrange("(kc p) n -> p kc n", p=128)[:, :, blk * 512:(blk + 1) * 512]
            return sv, dv, "%s_b%d_%d" % (kname, kq, blk)

        def bg_load(k):
            sv, _, _ = bg_views(k)
            P.add("act", lambda e, sv=sv: e.dma_start(out=bgf, in_=sv), writes=["ring3", "ring4"],
                  stream="ringbgl", nobar=True)

        def bg_step():
            k = bg_pos[0]
            if k >= len(bg_list):
                return
            bg_pos[0] += 1
            _, dv, wkey = bg_views(k)
            P.add("dve", lambda e: e.tensor_copy(out=bgb, in_=bgf), reads=["ring3", "ring4"],
                  writes=["ring5"], nobar=True)
            P.add("act", lambda e, dv=dv: e.dma_start(out=dv, in_=bgb), reads=["ring5"],
                  writes=[wkey], stream="ringbgs", nobar=True)
            if k + 1 < len(bg_list):
                bg_load(k + 1)
        P.add("pool", lambda e: e.dma_start(out=wpool_b, in_=wpool_d), writes=["wpool"], stream="cstp")
        for (t, d_, k) in ((convw, convw_d, "convw"), (convb, convb_d, "convb"), (poolsc, poolsc_d, "poolsc"),
                           (normw, normw_d, "normw"), (ln1g, ln1g_d, "ln1g"), (ln1b, ln1b_d, "ln1b"),
                           (ln2g, ln2g_d, "ln2g"), (ln2b, ln2b_d, "ln2b"), (dtb, dtb_d, "dtb"),
                           (aneg, alog_d, "aneg"), (dsk, dsk_d, "dsk")):
            P.add("sp", lambda e, t=t, d_=d_: e.dma_start(out=t, in_=d_), writes=[k, "cst_all"], stream="cst")
        P.add("pool", lambda e: e.memset(ident_f, 1.0), writes=["ident_f"])
        P.add("pool", lambda e: e.affine_select(out=ident_f, in_=ident_f, pattern=[[-1, 128]],
                                                compare_op=ALU.is_equal, fill=0.0, base=0, channel_multiplier=1),
              reads=["ident_f"], writes=["ident_f"])
        P.add("pool", lambda e: e.memset(U_f, 1.0), writes=["U_f"])
        P.add("pool", lambda e: e.affine_select(out=U_f, in_=U_f, pattern=[[1, 128]],
                                                compare_op=ALU.is_ge, fill=0.0, base=0, channel_multiplier=-1),
              reads=["U_f"], writes=["U_f"])
        P.add("pool", lambda e: e.memset(SL_f, 1.0), writes=["SL_f"])
        P.add("pool", lambda e: e.affine_select(out=SL_f, in_=SL_f, pattern=[[-1, 128]],
                                                compare_op=ALU.is_gt, fill=0.0, base=0, channel_multiplier=1),
              reads=["SL_f"], writes=["SL_f"])
        P.add("pool", lambda e: e.memset(ones_f, 1.0), writes=["ones_f"])
        P.add("pool", lambda e: e.memset(ones_b, 1.0), writes=["ones_b"])
        P.add("pool", lambda e: e.memset(negh, -0.5), writes=["negh"])
        P.add("pool", lambda e: e.memset(epsln, LN_EPS), writes=["epsln"])
        P.add("dve", lambda e: e.tensor_copy(out=ident_b, in_=ident_f), reads=["ident_f"], writes=["ident_b"])
        P.add("dve", lambda e: e.tensor_copy(out=U_b, in_=U_f), reads=["U_f"], writes=["U_b"])
        P.add("dve", lambda e: e.tensor_copy(out=SL_r, in_=SL_f), reads=["SL_f"], writes=["SL_r"])
        P.add("act", lambda e: e.activation(out=aneg, in_=aneg, func=AF.Exp), reads=["aneg", "cst_all"], writes=["aneg"])
        P.add("dve", lambda e: e.tensor_scalar(out=aneg, in0=aneg, scalar1=-1.0, scalar2=None, op0=ALU.mult),
              reads=["aneg"], writes=["aneg"])
        for q in range(80):
            P.add("dve", lambda e, q=q: e.tensor_scalar(out=convdiag[:, q, :], in0=ident_b,
                                                        scalar1=convw[:, q:q + 1], scalar2=None, op0=ALU.mult),
                  reads=["ident_b", "convw", "cst_all"], writes=["convdiag"])
        bg_load(0)
        P.barrier(lambda e: e.memset(bart, 0.0))

        def layer_norm(base, g_t, b_t, hT_out=None, after_fc=None):
            rb = carve_at(base, [8, 512], BF16)
            sqb = carve_at(base + 8192, [8, 512], BF16)
            mean = carve_at(base + 16384, [512], F32)
            rstd = carve_at(base + 18432, [512], F32)
            m2 = carve_at(base + 20480, [512], F32)
            b1, b2 = next_bank(), next_bank()
            for fc in range(8):
                P.add("act", lambda e, fc=fc: e.activation(out=rb[:, fc, :], in_=resid[:, fc, :], func=AF.Copy),
                      reads=["resid%d" % fc], writes=["rb%d" % fc])
                if fc % 2 == 1:
                    P.add("dve", lambda e, fc=fc: e.tensor_tensor(out=sqb[:, fc, :], in0=resid[:, fc, :],
                                                                  in1=resid[:, fc, :], op=ALU.mult),
                          reads=["resid%d" % fc], writes=["sqb%d" % fc])
                else:
                    P.add("act", lambda e, fc=fc: e.activation(out=sqb[:, fc, :], in_=resid[:, fc, :], func=AF.Square),
                          reads=["resid%d" % fc], writes=["sqb%d" % fc])
            for fc in range(8):
                P.add("pe", lambda e, fc=fc: e.matmul(pst[b1][:], lhsT=ones_b, rhs=rb[:, fc, :],
                                                      start=(fc == 0), stop=(fc == 7)),
                      reads=["rb%d" % fc, "ones_b"], writes=[psk(b1)])
            for fc in range(8):
                P.add("pe", lambda e, fc=fc: e.matmul(pst[b2][:], lhsT=ones_b, rhs=sqb[:, fc, :],
                                                      start=(fc == 0), stop=(fc == 7)),
                      reads=["sqb%d" % fc, "ones_b"], writes=[psk(b2)])
            P.add("act", lambda e: e.activation(out=mean, in_=pst[b1][:], func=AF.Copy, scale=1.0 / D),
                  writes=[psk(b1), "ln_mean"])
            P.add("dve", lambda e: e.tensor_tensor(out=m2, in0=mean, in1=mean, op=ALU.mult),
                  reads=["ln_mean"], writes=["ln_m2"])
            P.add("dve", lambda e: e.scalar_tensor_tensor(out=rstd, in0=pst[b2][:], scalar=1.0 / D, in1=m2,
                                                          op0=ALU.mult, op1=ALU.subtract),
                  reads=["ln_m2"], writes=[psk(b2), "ln_rstd"])
            P.add("act", lambda e: e.activation(out=m2, in_=rstd, func=AF.Ln, bias=epsln[:, 0:1]),
                  reads=["ln_rstd"], writes=["ln_m2"])
            P.add("act", lambda e: e.activation(out=rstd, in_=m2, func=AF.Exp, scale=-0.5),
                  reads=["ln_m2"], writes=["ln_rstd"])
            def gain(fc):
                P.add("dve", lambda e, fc=fc: e.tensor_scalar(out=resid[:, fc, :], in0=resid[:, fc, :],
                                                              scalar1=g_t[:, fc:fc + 1], scalar2=b_t[:, fc:fc + 1],
                                                              op0=ALU.mult, op1=ALU.add),
                      reads=["resid%d" % fc], writes=["resid%d" % fc])
            for fc in range(8):
                k = "resid%d" % fc
                P.add("dve", lambda e, fc=fc: e.tensor_tensor(out=resid[:, fc, :], in0=resid[:, fc, :], in1=mean,
                                                              op=ALU.subtract),
                      reads=[k, "ln_mean"], writes=[k])
                P.add("dve", lambda e, fc=fc: e.tensor_tensor(out=resid[:, fc, :], in0=resid[:, fc, :], in1=rstd,
                                                              op=ALU.mult),
                      reads=[k, "ln_rstd"], writes=[k])
                if hT_out is not None:
                    P.add("act", lambda e, fc=fc: e.activation(out=hT_out[:, fc, :], in_=resid[:, fc, :],
                                                               func=AF.Identity, scale=g_t[:, fc:fc + 1],
                                                               bias=b_t[:, fc:fc + 1]),
                          reads=[k], writes=["hTb%d" % fc])
                else:
                    P.add("act", lambda e, fc=fc: e.activation(out=resid[:, fc, :], in_=resid[:, fc, :],
                                                               func=AF.Identity, scale=g_t[:, fc:fc + 1],
                                                               bias=b_t[:, fc:fc + 1]),
                          reads=[k], writes=[k])
                    if after_fc is not None:
                        after_fc(fc)
            if hT_out is not None:
                for fc in range(8):
                    gain(fc)

        bar = lambda: P.barrier(lambda e: e.memset(bart, 0.0))

        for ti in range(NT):
            j = ti % (SEQ // TT)
            ring_n[0] = 3 if ti == 0 else NRING
            row0 = ti * TT
            def load_x(r0):
                for st in range(4):
                    P.add("sp", lambda e, st=st, r0=r0: e.dma_start(out=x_tok[:, st, :],
                                                             in_=x_d[r0 + st * 128: r0 + (st + 1) * 128, :]),
                          writes=["x_tok%d" % st], stream="xin%d" % st, nobar=True)
            if ti == 0:
                load_x(0)
            for st in range(4):
                if st % 2 == 0:
                    P.add("act", lambda e, st=st: e.activation(out=x_bf[:, st, :], in_=x_tok[:, st, :], func=AF.Copy),
                          reads=["x_tok%d" % st], writes=["x_bf%d" % st])
                else:
                    P.add("dve", lambda e, st=st: e.tensor_copy(out=x_bf[:, st, :], in_=x_tok[:, st, :]),
                          reads=["x_tok%d" % st], writes=["x_bf%d" % st])
            for fc in range(8):
                b = next_bank()
                pb = pst[b][:].bitcast(BF16)
                for st in range(4):
                    P.add("pe", lambda e, pb=pb, st=st, fc=fc: e.transpose(
                        pb[:, st * 128:(st + 1) * 128], x_bf[:, st, fc * 128:(fc + 1) * 128], ident_b),
                        reads=["x_bf%d" % st, "ident_b"], writes=[psk(b)])
                if fc % 2 == 0:
                    P.add("dve", lambda e, pb=pb, fc=fc: e.tensor_copy(out=xTb[:, fc, :], in_=pb[:, 0:512]),
                          writes=[psk(b), "xTb%d" % fc])
                else:
                    P.add("act", lambda e, pb=pb, fc=fc: e.activation(out=xTb[:, fc, :], in_=pb[:, 0:512],
                                                                      func=AF.Copy),
                          writes=[psk(b), "xTb%d" % fc])

            def resid_transposes(fc):
                b = next_bank()
                for st_ in range(4):
                    P.add("pe", lambda e, b=b, st_=st_, fc=fc: e.transpose(
                        pst[b][:, st_ * 128:(st_ + 1) * 128], x_tok[:, st_, fc * 128:(fc + 1) * 128], ident_f),
                        reads=["x_tok%d" % st_, "ident_f"], writes=[psk(b)])
                P.add("act", lambda e, b=b, fc=fc: e.activation(out=resid[:, fc, :], in_=pst[b][:], func=AF.Copy),
                      writes=[psk(b), "resid%d" % fc])
            if j == 0:
                P.add("pool", lambda e: e.memset(xbchalo, 0.0), writes=["xbchalo"])
                P.add("pool", lambda e: e.memset(poolhalo, 0.0), writes=["poolhalo"])
                P.add("pool", lambda e: e.memset(prevf, 0.0), writes=["prevf"])
                P.add("pool", lambda e: e.memset(prevb, 0.0), writes=["prevb"])
            xk = lambda kc: "xTb%d" % kc
            P.add("pool", lambda e: e.tensor_copy(out=poolA[:, :, 0:15], in_=poolhalo),
                  reads=["poolhalo"], writes=["poolA_h"])

            def evac_pool(c, b):
                P.add("act", lambda e, c=c, b=b: e.activation(out=poolA[:, c, 15:527], in_=pst[b][:], func=AF.Copy),
                      writes=[psk(b), "poolA%d" % c])
            gemm(wq_in, "wq_in", D, 0, 512, xTb, xk, evac_pool, kc_outer_first=True)
            P.add("pool", lambda e: e.tensor_copy(out=poolhalo, in_=poolA[:, :, 512:527]),
                  reads=["poolA%d" % c for c in range(4)] + ["poolA_h"], writes=["poolhalo"])
            for g, w in enumerate(WINS):
                src = poolA[:, g, :]
                bufs = [poolB, poolC]
                lv = int(math.log2(w))
                cur = src
                curk = ["poolA%d" % g, "poolA_h"]
                sh = 1
                for l in range(lv):
                    dst = bufs[l % 2]
                    dk = "poolBC%d" % (l % 2)
                    lo = 2 * sh - 1
                    P.add("pool", lambda e, dst=dst, cur=cur, lo=lo, sh=sh: e.tensor_tensor(
                        out=dst[:, lo:527], in0=cur[:, lo:527], in1=cur[:, lo - sh:527 - sh], op=ALU.add),
                        reads=curk, writes=[dk])
                    cur, curk, sh = dst, [dk], sh * 2
                P.add("dve", lambda e, g=g, cur=cur, w=w: e.scalar_tensor_tensor(
                    out=diff[:, g, :], in0=cur[:, 15:527], scalar=1.0 / w, in1=poolA[:, g, 15:527],
                    op0=ALU.mult, op1=ALU.subtract),
                    reads=curk + ["poolA%d" % g], writes=["diff%d" % g])
                if j == 0:
                    for t in range(w - 1):
                        P.add("dve", lambda e, g=g, cur=cur, t=t: e.scalar_tensor_tensor(
                            out=diff[:, g, t:t + 1], in0=cur[:, 15 + t:16 + t], scalar=1.0 / (t + 1),
                            in1=poolA[:, g, 15 + t:16 + t], op0=ALU.mult, op1=ALU.subtract),
                            reads=curk + ["poolA%d" % g, "diff%d" % g], writes=["diff%d" % g])
            for zb in range(3):
                s = ring_load(wq_in, "wq_in", 0, 512 + zb * 512, 512)
                for st in range(4):
                    b = next_bank()
                    for kc in range(8):
                        P.add("pe", lambda e, b=b, s=s, kc=kc, st=st: e.matmul(
                            pst[b][:], lhsT=xTb[:, kc, st * 128:(st + 1) * 128], rhs=ring[s][:, kc, :],
                            start=(kc == 0), stop=(kc == 7)),
                            reads=["ring%d" % s, xk(kc)], writes=[psk(b)])
                    P.add("act", lambda e, b=b, st=st, zb=zb: e.activation(
                        out=sz[:, st, zb * 512:(zb + 1) * 512], in_=pst[b][:], func=AF.Silu),
                        writes=[psk(b), "sz%d_%d" % (st, zb)])
                if ti == 0:
                    bg_step()
            def pool_matmuls():
                for g in range(4):
                    b = next_bank()
                    P.add("pe", lambda e, g=g, b=b: e.matmul(pst[b][:], lhsT=wpool_b[:, g, :], rhs=diff[:, g, :],
                                                             start=True, stop=True),
                          reads=["diff%d" % g, "wpool"], writes=[psk(b)])
                    P.add("act", lambda e, g=g, b=b: e.activation(out=mixedT[:, g, :], in_=pst[b][:], func=AF.Copy,
                                                                  scale=poolsc[:, g:g + 1]),
                          reads=["poolsc"], writes=[psk(b), "mixedT%d" % g])
            pool_matmuls()
            ectr = [0]

            def evac_xbc(c, b):
                ei = ectr[0] % 2
                ectr[0] += 1
                ek = "ext%d" % ei
                P.add("dve", lambda e, b=b, ei=ei: e.tensor_copy(out=ext[ei][:, 3:515], in_=pst[b][:]),
                      writes=[psk(b), ek])
                P.add("pool", lambda e, c=c, ei=ei: e.tensor_copy(out=ext[ei][:, 0:3], in_=xbchalo[:, c, :]),
                      reads=["xbchalo"], writes=[ek + "h"])
                P.add("pool", lambda e, c=c, ei=ei: e.tensor_copy(out=xbchalo[:, c, :], in_=ext[ei][:, 512:515]),
                      reads=[ek], writes=["xbchalo"])
                b2 = next_bank()
                for k in range(4):
                    P.add("pe", lambda e, b2=b2, c=c, k=k, ei=ei: e.matmul(
                        pst[b2][:], lhsT=convdiag[:, c * 4 + k, :], rhs=ext[ei][:, k:k + 512],
                        start=(k == 0), stop=(k == 3)),
                        reads=[ek, ek + "h", "convdiag"], writes=[psk(b2)])
                P.add("act", lambda e, b2=b2, c=c: e.activation(out=xbcT[:, c, :], in_=pst[b2][:], func=AF.Silu,
                                                                bias=convb[:, c:c + 1]),
                      reads=["convb"], writes=[psk(b2), "xbcT%d" % c])
                if ti == 0 and c % 4 == 3:
                    bg_step()
            gemm(wq_in, "wq_in", D, 2048, 2560, xTb, xk, evac_xbc)
            s = ring_load(wq_in, "wq_in", 0, 4608, 24)
            b = next_bank()
            for st in range(4):
                for kc in range(8):
                    P.add("pe", lambda e, b=b, s=s, kc=kc, st=st: e.matmul(
                        pst[b][:, st * 32:st * 32 + 24], lhsT=xTb[:, kc, st * 128:(st + 1) * 128],
                        rhs=ring[s][:, kc, 0:24], start=(kc == 0), stop=(kc == 7)),
                        reads=["ring%d" % s, xk(kc)], writes=[psk(b)])
            P.add("dve", lambda e, b=b: e.tensor_tensor(
                out=dtmp, in0=pst[b][:, 0:128].rearrange("p (a c) -> p a c", a=4)[:, :, 0:24],
                in1=dtb.unsqueeze(1).to_broadcast([128, 4, 24]), op=ALU.add),
                reads=["dtb"], writes=[psk(b), "dtmp"])
            P.add("act", lambda e: e.activation(out=dexp, in_=dtmp, func=AF.Exp), reads=["dtmp"], writes=["dexp"])
            P.add("act", lambda e: e.activation(out=dt_t, in_=dexp, func=AF.Ln, bias=1.0),
                  reads=["dexp"], writes=["dt_t"])
            P.add("dve", lambda e: e.tensor_tensor(out=a_t, in0=dt_t, in1=aneg.unsqueeze(1).to_broadcast([128, 4, 24]),
                                                   op=ALU.mult),
                  reads=["dt_t", "aneg"], writes=["a_t"])
            if ti == 0:
                dump("d_xbcT", xbcT, [128, 20, 512])
                dump("d_sz", sz, [128, 4, 1536])
                dump("d_dt", dt_t, [128, 4, 24])
                dump("d_pool", mixedT[:, 0:4, :], [128, 4, 512])
                dump("d_xT", xTb, [128, 8, 512])
            bar()
            if ti == 0:
                while bg_pos[0] < len(bg_list):
                    bg_step()
                ring_n[0] = NRING
            bank_pool[0] = [0, 1, 2, 3, 4, 5]
            def stageA(st):
                tsl = slice(st * 128, (st + 1) * 128)
                resid_transposes(2 * st)
                resid_transposes(2 * st + 1)
                for half in range(2):
                    b = next_bank()
                    pb = pst[b][:].bitcast(BF16)
                    for q in range(6):
                        P.add("pe", lambda e, pb=pb, q=q, half=half, tsl=tsl: e.transpose(
                            pb[:, q * 128:(q + 1) * 128], xbcT[:, half * 6 + q, tsl], ident_b),
                            reads=["xbcT%d" % (half * 6 + q), "ident_b"], writes=[psk(b)])
                    hs = slice(12 * half, 12 * half + 12)
                    pv = pb[:, 0:768].rearrange("p (h d) -> p h d", d=64)
                    P.add("dve", lambda e, pv=pv, hs=hs, st=st: e.tensor_tensor(
                        out=xdt[:, hs, :], in0=pv, in1=dt_t[:, st, hs].unsqueeze(2).to_broadcast([128, 12, 64]),
                        op=ALU.mult), reads=["dt_t"], writes=[psk(b), "xdt%d" % half])
                    P.add("dve", lambda e, pv=pv, hs=hs: e.tensor_tensor(
                        out=xsD[:, hs, :], in0=pv, in1=dsk[:, hs].unsqueeze(2).to_broadcast([128, 12, 64]),
                        op=ALU.mult), reads=["dsk"], writes=[psk(b), "xsD%d" % half])
                b = next_bank()
                pb = pst[b][:].bitcast(BF16)
                for g in range(4):
                    P.add("pe", lambda e, pb=pb, g=g, tsl=tsl: e.transpose(
                        pb[:, g * 128:(g + 1) * 128], xbcT[:, 12 + g, tsl], ident_b),
                        reads=["xbcT%d" % (12 + g), "ident_b"], writes=[psk(b)])
                P.add("act", lambda e, pb=pb: e.activation(out=Btok, in_=pb[:, 0:512].rearrange("p (g n) -> p g n", g=4),
                                                           func=AF.Copy),
                      writes=[psk(b), "Btok"])
                b = next_bank()
                for g in range(4):
                    P.add("pe", lambda e, b=b, g=g, tsl=tsl: e.matmul(
                        pst[b][:, g * 128:(g + 1) * 128], lhsT=xbcT[:, 12 + g, tsl], rhs=xbcT[:, 16 + g, tsl],
                        start=(g == 0), stop=(g == 3), skip_group_check=True),
                        reads=["xbcT%d" % (12 + g), "xbcT%d" % (16 + g)], writes=[psk(b)])
                P.add("dve", lambda e, b=b: e.tensor_tensor(
                    out=CBm2[st % 2], in0=pst[b][:].rearrange("p (g n) -> p g n", g=4),
                    in1=U_b.unsqueeze(1).to_broadcast([128, 4, 128]), op=ALU.mult),
                    reads=["U_b"], writes=[psk(b), "CBm%d" % (st % 2)])
                b = next_bank()
                P.add("pe", lambda e, b=b, st=st: e.matmul(pst[b][:, 0:24], lhsT=U_f, rhs=a_t[:, st, :],
                                                           start=True, stop=True, skip_group_check=True),
                      reads=["a_t", "U_f"], writes=[psk(b)])
                P.add("pe", lambda e, b=b, st=st: e.matmul(pst[b][:, 32:56], lhsT=ones_f, rhs=a_t[:, st, :],
                                                           start=False, stop=True, skip_group_check=True),
                      reads=["a_t", "ones_f"], writes=[psk(b)])
                P.add("pe", lambda e, b=b, st=st: e.matmul(pst[b][:, 64:88], lhsT=SL_f, rhs=a_t[:, st, :],
                                                           start=False, stop=True, skip_group_check=True),
                      reads=["a_t", "SL_f"], writes=[psk(b)])
                P.add("act", lambda e, b=b: e.activation(out=ea, in_=pst[b][:, 0:24], func=AF.Exp),
                      writes=[psk(b), "ea"])
                P.add("act", lambda e, b=b: e.activation(out=cd, in_=pst[b][:, 32:56], func=AF.Exp),
                      writes=[psk(b), "cd"])
                P.add("act", lambda e, b=b: e.activation(out=dte, in_=pst[b][:, 64:88], func=AF.Exp),
                      writes=[psk(b), "dte"])
            stageA(0)
            for st in range(4):
                tsl = slice(st * 128, (st + 1) * 128)

                def emit_xdec():
                    P.add("pool", lambda e: e.tensor_tensor(out=xdec, in0=xdt,
                                                            in1=dte.unsqueeze(2).to_broadcast([128, 24, 64]),
                                                            op=ALU.mult),
                          reads=["xdt0", "xdt1", "dte"], writes=["xdec"])

                def tri_front(stx, q3):
                    g = q3 // 2
                    ri, mi = (stx * 8 + q3) % 2, (stx * 8 + q3) % 3
                    hs = slice(3 * q3, 3 * q3 + 3)
                    CBm = CBm2[stx % 2]
                    P.add("pool", lambda e, ri=ri, hs=hs, stx=stx: e.tensor_tensor(
                        out=R3[ri], in0=a_t[:, stx, hs].unsqueeze(2).to_broadcast([128, 3, 128]),
                        in1=U_f.unsqueeze(1).to_broadcast([128, 3, 128]), op=ALU.mult),
                        reads=["a_t", "U_f"], writes=["R3_%d" % ri])
                    b = next_bank()
                    P.add("pe", lambda e, b=b, ri=ri: e.matmul(
                        pst[b][:, 0:384], lhsT=SL_r, rhs=R3[ri][:].rearrange("p h l -> p (h l)"),
                        start=True, stop=True),
                        reads=["R3_%d" % ri, "SL_r"], writes=[psk(b)])
                    P.add("act", lambda e, b=b, ri=ri: e.activation(
                        out=L3[ri][:].rearrange("p h l -> p (h l)"), in_=pst[b][:, 0:384], func=AF.Exp),
                        writes=[psk(b), "L3_%d" % ri])
                    P.add("dve", lambda e, ri=ri, mi=mi, g=g, CBm=CBm: e.tensor_tensor(
                        out=M3[mi], in0=L3[ri], in1=CBm[:, g, :].unsqueeze(1).to_broadcast([128, 3, 128]),
                        op=ALU.mult),
                        reads=["L3_%d" % ri, "CBm%d" % (stx % 2)], writes=["M3_%d" % mi])

                def tri_back(q3):
                    g = q3 // 2
                    mi = (st * 8 + q3) % 3
                    yb = 6 + (g % 2)
                    for hh in range(3):
                        h = 3 * q3 + hh
                        hl = h % 6
                        P.add("pe", lambda e, yb=yb, mi=mi, hh=hh, h=h, hl=hl: e.matmul(
                            pst[yb][:, hl * 64:(hl + 1) * 64], lhsT=M3[mi][:, hh, :], rhs=xdt[:, h, :],
                            start=(hl == 0), stop=False, skip_group_check=True),
                            reads=["M3_%d" % mi, "xdt%d" % (h // 12)], writes=[psk(yb)])
                    if q3 % 2 == 1:
                        P.add("pe", lambda e, yb=yb, g=g: e.matmul(
                            pst[yb][:, 0:384], lhsT=ident_b,
                            rhs=xsD[:, 6 * g:6 * g + 6, :].rearrange("p h d -> p (h d)"),
                            start=False, stop=True, skip_group_check=True),
                            reads=["xsD%d" % (g // 2), "ident_b"], writes=[psk(yb)])

                def group_post(g):
                    hs6 = slice(6 * g, 6 * g + 6)
                    ti1 = g % 2
                    yb = 6 + (g % 2)
                    bo = next_bank()
                    P.add("pe", lambda e, bo=bo, g=g, tsl=tsl: e.matmul(
                        pst[bo][:, 0:384], lhsT=xbcT[:, 16 + g, tsl], rhs=prevb[:, g, :], start=True, stop=True),
                        reads=["xbcT%d" % (16 + g), "prevb%d" % g], writes=[psk(bo)])
                    bs = next_bank()
                    P.add("pe", lambda e, bs=bs, g=g, hs6=hs6: e.matmul(
                        pst[bs][:, 0:384], lhsT=Btok[:, g, :], rhs=xdec[:, hs6, :].rearrange("p h d -> p (h d)"),
                        start=True, stop=True),
                        reads=["Btok", "xdec"], writes=[psk(bs)])
                    P.add("dve", lambda e, bo=bo, ti1=ti1, hs6=hs6: e.tensor_tensor(
                        out=t1[ti1], in0=pst[bo][:, 0:384].rearrange("p (h d) -> p h d", d=64),
                        in1=ea[:, hs6].unsqueeze(2).to_broadcast([128, 6, 64]), op=ALU.mult),
                        reads=["ea"], writes=[psk(bo), "t1_%d" % ti1])
                    P.add("dve", lambda e, yb=yb, ti1=ti1: e.tensor_tensor(
                        out=t1[ti1], in0=pst[yb][:, 0:384].rearrange("p (h d) -> p h d", d=64), in1=t1[ti1],
                        op=ALU.add),
                        reads=["t1_%d" % ti1], writes=[psk(yb), "t1_%d" % ti1])
                    P.add("dve", lambda e, g=g, ti1=ti1, st=st: e.tensor_tensor(
                        out=vv[:, g, :], in0=t1[ti1][:].rearrange("p h d -> p (h d)"),
                        in1=sz[:, st, g * 384:(g + 1) * 384], op=ALU.mult),
                        reads=["t1_%d" % ti1] + ["sz%d_%d" % (st, zb) for zb in range(3)], writes=["vv%d" % g])
                    P.add("pool", lambda e, g=g, hs6=hs6, ti1=ti1: e.tensor_tensor(
                        out=t2[ti1], in0=prevf[:, g, :].rearrange("p (h d) -> p h d", d=64),
                        in1=cd[:, hs6].unsqueeze(2).to_broadcast([128, 6, 64]), op=ALU.mult),
                        reads=["prevf%d" % g, "cd"], writes=["t2_%d" % ti1])
                    P.add("dve", lambda e, bs=bs, g=g, ti1=ti1: e.tensor_tensor(
                        out=prevf[:, g, :], in0=pst[bs][:, 0:384], in1=t2[ti1][:].rearrange("p h d -> p (h d)"),
                        op=ALU.add),
                        reads=["t2_%d" % ti1], writes=[psk(bs), "prevf%d" % g])
                    if g > 0:
                        group_act(g - 1)

                def group_act(g):
                    P.add("act", lambda e, g=g: e.activation(out=junk, in_=vv[:, g, :], func=AF.Square,
                                                             accum_out=ss[:, g:g + 1]),
                          reads=["vv%d" % g], writes=["junk", "ss%d" % g])
                    P.add("act", lambda e, g=g: e.activation(out=prevb[:, g, :], in_=prevf[:, g, :], func=AF.Copy),
                          reads=["prevf%d" % g], writes=["prevb%d" % g])

                if st == 0:
                    tri_front(0, 0)
                    tri_front(0, 1)
                for q3 in range(8):
                    if q3 + 2 < 8:
                        tri_front(st, q3 + 2)
                    if q3 == 0:
                        emit_xdec()
                    tri_back(q3)
                    if q3 % 2 == 1:
                        group_post(q3 // 2)
                group_act(3)
                ssk = ["ss%d" % g for g in range(4)]
                P.add("dve", lambda e: e.tensor_scalar(out=ms, in0=ss, scalar1=1.0 / 384, scalar2=RMS_EPS,
                                                       op0=ALU.mult, op1=ALU.add), reads=ssk, writes=["ms"])
                P.add("act", lambda e: e.activation(out=lnms, in_=ms, func=AF.Ln), reads=["ms"], writes=["lnms"])
                P.add("act", lambda e: e.activation(out=rstd4, in_=lnms, func=AF.Exp, scale=-0.5),
                      reads=["lnms"], writes=["rstd4"])
                for g in range(4):
                    P.add("act", lambda e, g=g: e.activation(out=vn[:, g, :], in_=vv[:, g, :], func=AF.Copy,
                                                             scale=rstd4[:, g:g + 1]),
                          reads=["rstd4", "vv%d" % g], writes=["vn%d" % g])
                if st + 1 < 4:
                    stageA(st + 1)
                    tri_front(st + 1, 0)
                    tri_front(st + 1, 1)
                if ti == 0 and st == 0:
                    dump("d_vv0", vv, [128, 4, 384])
                    dump("d_ss", ss, [128, 4])
                    dump("d_rstd4", rstd4, [128, 4])
                    dump("d_ms", ms, [128, 4])
                    dump("d_lnms", lnms, [128, 4])
                    dump("d_vn", vn, [128, 4, 384])
                vnf = vn[:].rearrange("p g d -> p (g d)")
                for half in range(2):
                    b = next_bank()
                    pb = pst[b][:].bitcast(BF16)
                    for q in range(6):
                        c = half * 6 + q
                        P.add("pe", lambda e, pb=pb, q=q, c=c: e.transpose(
                            pb[:, q * 128:(q + 1) * 128], vnf[:, c * 128:(c + 1) * 128], ident_b),
                            reads=["vn%d" % (c // 3), "ident_b"], writes=[psk(b)])
                    P.add("dve", lambda e, pb=pb, half=half, tsl=tsl: e.tensor_tensor(
                        out=mixedT[:, 4 + 6 * half:10 + 6 * half, tsl],
                        in0=pb[:, 0:768].rearrange("p (c t) -> p c t", c=6),
                        in1=normw[:, 6 * half:6 * half + 6].unsqueeze(2).to_broadcast([128, 6, 128]), op=ALU.mult),
                        reads=["normw"], writes=[psk(b)] + ["mixedT%d" % (4 + 6 * half + q) for q in range(6)])
            if ti == 0:
                dump("d_mixedT", mixedT, [128, 16, 512])
                dump("d_vv", vv, [128, 4, 384])
                dump("d_prevf", prevf, [128, 4, 384])
            if ti + 1 < NT:
                load_x((ti + 1) * TT)
            bank_pool[0] = list(range(8))
            def evac_res(c, b):
                P.add("dve", lambda e, c=c, b=b: e.scalar_tensor_tensor(
                    out=resid[:, c, :], in0=resid[:, c, :], scalar=ALPHA, in1=pst[b][:],
                    op0=ALU.mult, op1=ALU.add), reads=["resid%d" % c], writes=[psk(b), "resid%d" % c])
            gemm(wq_out, "wq_out", 2048, 0, D, mixedT, lambda kc: "mixedT%d" % kc, evac_res)
            layer_norm(LN3, ln1g, ln1b, hT_out=hTb)
            if ti == 0:
                dump("d_h", resid, [128, 8, 512])
            rctr = [0]

            def evac_ff1(c, b):
                ri = rctr[0] % 2
                rctr[0] += 1
                P.add("act", lambda e, b=b, ri=ri: e.activation(out=rl[ri], in_=pst[b][:], func=AF.Relu),
                      writes=[psk(b), "rl%d" % ri])
                P.add("dve", lambda e, c=c, ri=ri: e.tensor_tensor(out=hid[:, c, :], in0=rl[ri], in1=rl[ri],
                                                                   op=ALU.mult),
                      reads=["rl%d" % ri], writes=["hid%d" % c])
            gemm(wq_ff1, "wq_ff1", D, 0, DFF, hTb, lambda kc: "hTb%d" % kc, evac_ff1, kc_outer_first=True)
            gemm(wq_ff2, "wq_ff2", DFF, 0, D, hid, lambda kc: "hid%d" % kc, evac_res)
            obanks = {}

            def evac_half(hf):
                for st_ in range(4):
                    b = obanks[(st_, hf)]
                    if st_ % 2 == 0:
                        P.add("act", lambda e, b=b, st_=st_, hf=hf: e.activation(
                            out=ostage4[st_][:, hf * 512:(hf + 1) * 512], in_=pst[b][:], func=AF.Copy),
                            writes=[psk(b), "ost%d_%d" % (st_, hf)])
                    else:
                        P.add("dve", lambda e, b=b, st_=st_, hf=hf: e.tensor_copy(
                            out=ostage4[st_][:, hf * 512:(hf + 1) * 512], in_=pst[b][:]),
                            writes=[psk(b), "ost%d_%d" % (st_, hf)])

            def out_fc(fc):
                hf, q = fc // 4, fc % 4
                if q == 0:
                    for st_ in range(4):
                        obanks[(st_, hf)] = next_bank()
                for st_ in range(4):
                    b = obanks[(st_, hf)]
                    P.add("pe", lambda e, b=b, q=q, fc=fc, st_=st_: e.transpose(
                        pst[b][:, q * 128:(q + 1) * 128], resid[:, fc, st_ * 128:(st_ + 1) * 128], ident_f),
                        reads=["resid%d" % fc, "ident_f"], writes=[psk(b)])
                if fc == 5:
                    evac_half(0)
            layer_norm(LN4, ln2g, ln2b, after_fc=out_fc)
            evac_half(1)
            for st in range(4):
                P.add("act", lambda e, st=st, row0=row0: e.dma_start(
                    out=out_d[row0 + st * 128: row0 + (st + 1) * 128, :], in_=ostage4[st]),
                    reads=["ost%d_0" % st, "ost%d_1" % st], stream="outst%d" % st)
            bar()
        P.emit(ctx)
        build_nc.stats = P.stats
    return nc


_NC = None


def kernel(x, w_in, w_pool, pool_scale, conv_w, conv_b, dt_bias, a_log, d_skip, ssd_norm_w, w_out,
           ln1_g, ln1_b, w_ff1, w_ff2, ln2_g, ln2_b):
    global _NC
    if _NC is None:
        _NC = build_nc()
    nc = _NC
    f = lambda a: np.ascontiguousarray(np.asarray(a, dtype=np.float32))
    x = f(x).reshape(16 * SEQ, D)
    rep = lambda v: f(np.broadcast_to(f(v).reshape(1, -1), (128, f(v).size)))
    colm = lambda v, n: f(f(v).reshape(n, 128).T)
    shared = {
        "w_in": f(w_in[0]), "w_out": f(w_out[0]), "w_ff1": f(w_ff1[0]), "w_ff2": f(w_ff2[0]),
        "w_pool_l": f(np.transpose(f(w_pool[0]), (1, 0, 2))),
        "conv_w_l": f(np.transpose(f(conv_w[0]).reshape(4, 20, 128), (2, 1, 0)).reshape(128, 80)),
        "conv_b_l": colm(conv_b[0], 20),
        "pool_scale_l": colm(pool_scale[0], 4),
        "ssd_norm_w_l": colm(ssd_norm_w[0], 12),
        "ln1_g_l": colm(ln1_g[0], 8), "ln1_b_l": colm(ln1_b[0], 8),
        "ln2_g_l": colm(ln2_g[0], 8), "ln2_b_l": colm(ln2_b[0], 8),
        "dt_bias_r": rep(dt_bias[0]), "a_log_r": rep(a_log[0]), "d_skip_r": rep(d_skip[0]),
    }
    in_maps = []
    for c in range(NCORES):
        m = dict(shared)
        m["x"] = np.ascontiguousarray(x[c * TOK:(c + 1) * TOK])
        in_maps.append(m)
    res = run_bass_kernel_spmd(nc, in_maps, core_ids=list(range(NCORES)))
    out = np.concatenate([np.asarray(r["out"], dtype=np.float32) for r in res.results], axis=0)
    return out.reshape(16, SEQ, D)
```

```python
import math
from contextlib import ExitStack
import numpy as np
import concourse.bass as bass
import concourse.mybir as mybir
from concourse.bass_utils import run_bass_kernel_spmd

F32 = mybir.dt.float32
BF16 = mybir.dt.bfloat16
F32R = mybir.dt.float32r
AF = mybir.ActivationFunctionType
ALU = mybir.AluOpType

NCORES = 8
D = 1024
TOK = 4096
TT = 512
NT = TOK // TT
SEQ = 2048
DIN = 4632
DFF = 4096
ALPHA = 2.0 ** 0.25
LN_EPS = 1e-5
RMS_EPS = 1e-5
WINS = (2, 4, 8, 16)

ENGS = ("pe", "act", "dve", "pool", "sp")


class Prog:
    def __init__(self, nc):
        self.nc = nc
        self.ops = []
        self.lastw = {}
        self.readers = {}
        self.streams = {}
        self.last_eng = {}
        self.last_stream = {}
        self.bar = None

    def _ekey(self, i):
        o = self.ops[i]
        return ("s", o["stream"]) if o["stream"] is not None else ("e", o["eng"])

    def add(self, eng, fn, reads=(), writes=(), stream=None, extra=(), nobar=False):
        i = len(self.ops)
        deps = set(extra)
        raw = set()
        for k in reads:
            w = self.lastw.get(k)
            if w is not None:
                deps.add(w)
                raw.add(w)
        for k in writes:
            w = self.lastw.get(k)
            if w is not None:
                deps.add(w)
            deps.update(self.readers.get(k, {}).values())
        if self.bar is not None and not nobar:
            deps.add(self.bar)
        fdeps = set()
        for d in deps:
            od = self.ops[d]
            if od["stream"] is None and stream is None and od["eng"] == eng:
                if eng == "pe" or d not in raw:
                    continue
            fdeps.add(d)
        sval = None
        if stream is not None:
            self.streams[stream] = self.streams.get(stream, 0) + 1
            sval = 16 * self.streams[stream]
        self.ops.append(dict(eng=eng, fn=fn, deps=fdeps, stream=stream, sval=sval, hasdep=False))
        for d in fdeps:
            self.ops[d]["hasdep"] = True
        ek = self._ekey(i)
        for k in reads:
            self.readers.setdefault(k, {})[ek] = i
        for k in writes:
            self.lastw[k] = i
            self.readers[k] = {}
        if stream is None:
            self.last_eng[eng] = i
        else:
            self.last_stream[stream] = i
        return i

    def barrier(self, fn):
        front = set(self.last_eng.values()) | set(
            v for k, v in self.last_stream.items() if not (k.startswith("wcast_") or k.startswith("ring")))
        self.bar = None
        b = self.add("pool", fn, extra=front)
        self.bar = b
        keep = lambda k: k.startswith("ring") or k.startswith("wq_") or k.startswith("x_tok")
        self.lastw = {k: v for k, v in self.lastw.items() if keep(k)}
        self.readers = {k: v for k, v in self.readers.items() if keep(k)}

    def emit(self, ctx):
        nc = self.nc
        ops = self.ops
        esem = {e: ctx.enter_context(nc.semaphore("e_" + e)) for e in ENGS}
        ssem = {s: ctx.enter_context(nc.semaphore("s_" + s)) for s in self.streams}
        cnt = {e: 0 for e in ENGS}
        for o in ops:
            if o["stream"] is None:
                if o["hasdep"]:
                    cnt[o["eng"]] += 1
                    o["tick"] = cnt[o["eng"]]
                else:
                    o["tick"] = None
        engvc = {e: {} for e in ENGS}
        for o in ops:
            vc = engvc[o["eng"]]
            wm = {}
            for d in sorted(o["deps"]):
                od = ops[d]
                if od["stream"] is not None:
                    key, val = ("s", od["stream"]), od["sval"]
                else:
                    key, val = ("e", od["eng"]), od["tick"]
                if vc.get(key, 0) < val:
                    wm[key] = max(wm.get(key, 0), val)
                    vc[key] = val
                for k2, v2 in od["vc"].items():
                    if vc.get(k2, 0) < v2:
                        vc[k2] = v2
            o["waits"] = wm
            snap = dict(vc)
            if o["stream"] is None and o["tick"] is not None:
                snap[("e", o["eng"])] = o["tick"]
            o["vc"] = snap
        final = [(("s", s), 16 * c) for s, c in self.streams.items()]
        self.stats = dict(nops=len(ops), nwaits=sum(len(o["waits"]) for o in ops), ticks=dict(cnt),
                          per_eng={e: sum(1 for o in ops if o["eng"] == e) for e in ENGS})

        def sem_of(key):
            return ssem[key[1]] if key[0] == "s" else esem[key[1]]

        def run(engname):
            def body(eng):
                for o in ops:
                    if o["eng"] != engname:
                        continue
                    for k, v in o["waits"].items():
                        eng.wait_ge(sem_of(k), v)
                    ins = o["fn"](eng)
                    if o["stream"] is not None:
                        ins.then_inc(ssem[o["stream"]], 16)
                    elif o["tick"] is not None:
                        ins.then_inc(esem[engname], 1)
                if engname == "sp":
                    for k, v in final:
                        eng.wait_ge(sem_of(k), v)
            return body

        with nc.Block() as block:
            block.tensor(run("pe"))
            block.scalar(run("act"))
            block.vector(run("dve"))
            block.gpsimd(run("pool"))
            block.sync(run("sp"))


DEBUG = False


def build_nc():
    nc = bass.Bass("TRN2", target_bir_lowering=False)
    dbg = {}

    def dbg_out(name, shape):
        dbg[name] = nc.dram_tensor(name, list(shape), F32, kind="ExternalOutput").ap()
        return dbg[name]

    def din(name, shape):
        return nc.dram_tensor(name, list(shape), F32, kind="ExternalInput").ap()

    x_d = din("x", [TOK, D])
    w_in_d = din("w_in", [D, DIN])
    w_out_d = din("w_out", [2048, D])
    w_ff1_d = din("w_ff1", [D, DFF])
    w_ff2_d = din("w_ff2", [DFF, D])
    wpool_d = din("w_pool_l", [128, 4, 128])
    convw_d = din("conv_w_l", [128, 80])
    convb_d = din("conv_b_l", [128, 20])
    poolsc_d = din("pool_scale_l", [128, 4])
    normw_d = din("ssd_norm_w_l", [128, 12])
    ln1g_d = din("ln1_g_l", [128, 8])
    ln1b_d = din("ln1_b_l", [128, 8])
    ln2g_d = din("ln2_g_l", [128, 8])
    ln2b_d = din("ln2_b_l", [128, 8])
    dtb_d = din("dt_bias_r", [128, 24])
    alog_d = din("a_log_r", [128, 24])
    dsk_d = din("d_skip_r", [128, 24])
    out_d = nc.dram_tensor("out", [TOK, D], F32, kind="ExternalOutput").ap()
    wq_in = nc.dram_tensor("wq_in", [D, DIN], BF16, kind="Internal").ap()
    wq_out = nc.dram_tensor("wq_out", [2048, D], BF16, kind="Internal").ap()
    wq_ff1 = nc.dram_tensor("wq_ff1", [D, DFF], BF16, kind="Internal").ap()
    wq_ff2 = nc.dram_tensor("wq_ff2", [DFF, D], BF16, kind="Internal").ap()

    with ExitStack() as ctx:
        TOTAL = 206848
        big = ctx.enter_context(nc.sbuf_tensor("big", [128, TOTAL // 2], BF16))
        pst = [ctx.enter_context(nc.psum_tensor("ps%d" % i, [128, 512], F32)) for i in range(8)]
        state = dict(off=0)

        def carve_at(off, free, dt):
            nb = int(np.prod(free)) * (2 if dt == BF16 else 4)
            assert off % 4 == 0 and off + nb <= TOTAL, (off, nb)
            v = big[:, off // 2:(off + nb) // 2]
            if dt != BF16:
                v = v.bitcast(dt)
            if len(free) == 2:
                v = v.rearrange("p (a b) -> p a b", a=free[0])
            return v

        def carve(free, dt):
            nb = int(np.prod(free)) * (2 if dt == BF16 else 4)
            nb = (nb + 63) // 64 * 64
            v = carve_at(state["off"], free, dt)
            state["off"] += nb
            return v

        convdiag = carve([80, 128], BF16)
        NRING = 6
        ring_off = state["off"]
        ring = [carve([8, 512], BF16) for _ in range(NRING)]
        bgf = carve_at(ring_off + 3 * 8192, [8, 512], F32)
        bgb = ring[5]
        x_tok = carve([4, 1024], F32)
        resid = carve([8, 512], F32)
        ident_b = carve([128], BF16)
        ident_f = carve([128], F32)
        U_f = carve([128], F32)
        ones_f = carve([128], F32)
        SL_f = carve([128], F32)
        SL_r = ctx.enter_context(nc.sbuf_tensor("SL_r", [128, 128], F32R))[:]
        U_b = carve([128], BF16)
        ones_b = carve([128], BF16)
        negh = carve([512], F32)
        epsln = carve([16], F32)
        convw = carve([80], F32)
        convb = carve([20], F32)
        poolsc = carve([4], F32)
        normw = carve([12], F32)
        ln1g = carve([8], F32)
        ln1b = carve([8], F32)
        ln2g = carve([8], F32)
        ln2b = carve([8], F32)
        dtb = carve([24], F32)
        aneg = carve([24], F32)
        dsk = carve([24], F32)
        wpool_b = carve([4, 128], BF16)
        xbchalo = carve([20, 3], BF16)
        poolhalo = carve([4, 15], F32)
        prevf = carve([4, 384], F32)
        prevb = carve([4, 384], BF16)
        bart = carve([16], F32)
        S = state["off"]
        SCR = TOTAL - S
        assert SCR >= 81920, SCR

        xbcT = carve_at(S + 0, [20, 512], BF16)
        sz = carve_at(S + 20480, [4, 1536], BF16)
        mixedT = carve_at(S + 32768, [16, 512], BF16)
        dt_t = carve_at(S + 49152, [4, 24], F32)
        a_t = carve_at(S + 49152 + 384, [4, 24], F32)
        T0 = S + 49920
        xTb = carve_at(T0, [8, 512], BF16)
        o1 = T0 + 8192
        poolA = carve_at(o1, [4, 527], F32); o1 += 8448
        poolB = carve_at(o1, [527], F32); o1 += 2112
        poolC = carve_at(o1, [527], F32); o1 += 2112
        diff = carve_at(o1, [4, 512], BF16); o1 += 4096
        ext = [carve_at(o1, [516], BF16), carve_at(o1 + 1040, [516], BF16)]; o1 += 2080
        dtmp = carve_at(o1, [4, 24], F32); o1 += 384
        dexp = carve_at(o1, [4, 24], F32); o1 += 384
        x_bf = carve_at(o1, [4, 1024], BF16); o1 += 8192
        assert o1 <= TOTAL, (o1, TOTAL)
        o2 = T0
        xdt = carve_at(o2, [24, 64], BF16); o2 += 3072
        xsD = carve_at(o2, [24, 64], BF16); o2 += 3072
        xdec = carve_at(o2, [24, 64], BF16); o2 += 3072
        Btok = carve_at(o2, [4, 128], BF16); o2 += 1024
        CBm2 = [carve_at(o2, [4, 128], BF16), carve_at(o2 + 1024, [4, 128], BF16)]; o2 += 2048
        R3 = [ctx.enter_context(nc.sbuf_tensor("R3_%d" % i, [128, 384], F32R))[:].rearrange("p (h l) -> p h l", h=3)
              for i in range(2)]
        L3 = [carve_at(o2, [3, 128], BF16), carve_at(o2 + 768, [3, 128], BF16)]; o2 += 1536
        M3 = [carve_at(o2 + 768 * m, [3, 128], BF16) for m in range(3)]; o2 += 2304
        t1 = [carve_at(o2, [6, 64], F32), carve_at(o2 + 1536, [6, 64], F32)]; o2 += 3072
        t2 = [carve_at(o2, [6, 64], F32), carve_at(o2 + 1536, [6, 64], F32)]; o2 += 3072
        vv = carve_at(o2, [4, 384], F32); o2 += 6144
        vn = carve_at(o2, [4, 384], BF16); o2 += 3072
        junk = carve_at(o2, [384], F32); o2 += 1536
        acs = carve_at(o2, [24], F32); o2 += 128
        ea = carve_at(o2, [24], F32); o2 += 128
        cd = carve_at(o2, [24], F32); o2 += 128
        dte = carve_at(o2, [24], F32); o2 += 128
        tmpd = carve_at(o2, [24], F32); o2 += 128
        ss = carve_at(o2, [4], F32); o2 += 64
        ms = carve_at(o2, [4], F32); o2 += 64
        rstd4 = carve_at(o2, [4], F32); o2 += 64
        lnms = carve_at(o2, [4], F32); o2 += 64
        assert o2 <= TOTAL
        hTb = carve_at(S + 0, [8, 512], BF16)
        hid = carve_at(S + 8192, [32, 512], BF16)
        rl = [carve_at(S + 40960, [512], BF16), carve_at(S + 41984, [512], BF16)]
        ostage = [carve_at(S + 43008, [1024], F32), carve_at(S + 47104, [1024], F32)]
        LN3 = T0
        LN4 = S + 51200
        assert LN4 + 22528 <= TOTAL and LN3 + 22528 <= TOTAL

        P = Prog(nc)
        bank_ctr = [0]

        def dump(name, ap, shape):
            if not DEBUG:
                return
            d = dbg_out(name, shape)
            front = set(P.last_eng.values()) | set(v for k, v in P.last_stream.items() if not k.startswith("ring"))
            i = P.add("pool", lambda e, d=d, ap=ap: e.dma_start(out=d, in_=ap), stream="dbg", extra=front)
            P.add("pool", lambda e: e.memset(bart, 0.0), extra=[i])
            P.barrier(lambda e: e.memset(bart, 0.0))

        bank_pool = [list(range(8))]

        def next_bank():
            pool_ = bank_pool[0]
            b = pool_[bank_ctr[0] % len(pool_)]
            bank_ctr[0] += 1
            return b

        def psk(b):
            return "ps%d" % b

        ring_ctr = [0]
        ring_n = [3]

        def ring_load(wq, wkey, kq, c0, n):
            s = ring_ctr[0] % ring_n[0]
            ring_ctr[0] += 1
            wkey = "%s_b%d_%d" % (wkey, kq, c0 // 512)

            src = wq.rearrange("(kc p) n -> p kc n", p=128)[:, kq * 8:(kq + 1) * 8, c0:c0 + n]
            P.add("sp", lambda e, s=s, src=src, n=n: e.dma_start(out=ring[s][:, :, 0:n], in_=src),
                  reads=[wkey], writes=["ring%d" % s], stream="ring%d" % s, nobar=True)
            return s

        def gemm(wq, wkey, K, c0, ncols, rhsT, rkeys, evac, kc_outer_first=False):
            nkq = K // 1024
            for nb in range((ncols + 511) // 512):
                n = min(512, ncols - nb * 512)
                nch = n // 128
                banks = [next_bank() for _ in range(nch)]
                for kq in range(nkq):
                    s = ring_load(wq, wkey, kq, c0 + nb * 512, n)
                    order = [(fcl, kc) for fcl in range(nch) for kc in range(8)]
                    if kc_outer_first and nb == 0:
                        order = [(fcl, kc) for kc in range(8) for fcl in range(nch)]
                    for fcl, kc in order:
                        if True:
                            P.add("pe", lambda e, b=banks[fcl], s=s, kc=kc, fcl=fcl, kq=kq:
                                  e.matmul(pst[b][:], lhsT=ring[s][:, kc, fcl * 128:(fcl + 1) * 128],
                                           rhs=rhsT[:, kq * 8 + kc, :],
                                           start=(kq == 0 and kc == 0), stop=(kq == nkq - 1 and kc == 7)),
                                  reads=["ring%d" % s, rkeys(kq * 8 + kc)], writes=[psk(banks[fcl])])
                for fcl in range(nch):
                    evac(nb * 4 + fcl, banks[fcl])

        def cast_w(src, dst, rows, key, extra=()):
            for r in range(rows // 128):
                P.add("pool", lambda e, r=r: e.dma_start(out=dst[r * 128:(r + 1) * 128, :],
                                                         in_=src[r * 128:(r + 1) * 128, :]),
                      writes=[key], stream="wcast_" + key, nobar=True, extra=extra)
        stg_f = [carve_at(S + i * 16384, [8, 512], F32) for i in range(3)]
        stg_b = [carve_at(S + 49152 + i * 8192, [8, 512], BF16) for i in range(3)]
        sctr = [0]

        def stage_cast(src, dst, K, ncols, kname):
            for kq in range(K // 1024):
                for blk in range((ncols + 511) // 512):
                    n = min(512, ncols - blk * 512)
                    i = sctr[0] % 3
                    sctr[0] += 1
                    sv = src[kq * 1024:(kq + 1) * 1024, :].rearrange("(kc p) n -> p kc n", p=128)[
                        :, :, blk * 512:blk * 512 + n]
                    dv = dst[kq * 1024:(kq + 1) * 1024, :].rearrange("(kc p) n -> p kc n", p=128)[
                        :, :, blk * 512:blk * 512 + n]
                    P.add("sp", lambda e, i=i, sv=sv, n=n: e.dma_start(out=stg_f[i][:, :, 0:n], in_=sv),
                          writes=["stgf%d" % i], stream="stgl%d" % i)
                    ce = ("dve", "act")[sctr[0] % 2]
                    if ce == "dve":
                        P.add("dve", lambda e, i=i, n=n: e.tensor_copy(out=stg_b[i][:, :, 0:n],
                                                                       in_=stg_f[i][:, :, 0:n]),
                              reads=["stgf%d" % i], writes=["stgb%d" % i])
                    else:
                        P.add("act", lambda e, i=i, n=n: e.activation(out=stg_b[i][:, :, 0:n],
                                                                      in_=stg_f[i][:, :, 0:n], func=AF.Copy),
                              reads=["stgf%d" % i], writes=["stgb%d" % i])
                    P.add("act", lambda e, i=i, dv=dv, n=n: e.dma_start(out=dv, in_=stg_b[i][:, :, 0:n]),
                          reads=["stgb%d" % i], writes=["%s_b%d_%d" % (kname, kq, blk)], stream="stgs%d" % i)
        stage_cast(w_in_d, wq_in, D, DIN, "wq_in")
        stage_cast(w_out_d, wq_out, 2048, D, "wq_out")
        stage_cast(w_ff1_d, wq_ff1, D, DFF, "wq_ff1")
        bg_list = [(w_ff2_d, wq_ff2, "wq_ff2", kq, blk) for blk in range(2) for kq in range(4)]
        bg_pos = [0]

        def bg_views(k):
            src, dst, kname, kq, blk = bg_list[k]
            sv = src[kq * 1024:(kq + 1) * 1024, :].rearrange("(kc p) n -> p kc n", p=128)[:, :, blk * 512:(blk + 1) * 512]
            dv = dst[kq * 1024:(kq + 1) * 1024, :].rearrange("(kc p) n -> p kc n", p=128)[:, :, blk * 512:(blk + 1) * 512]
            return sv, dv, "%s_b%d_%d" % (kname, kq, blk)

        def bg_load(k):
            sv, _, _ = bg_views(k)
            P.add("act", lambda e, sv=sv: e.dma_start(out=bgf, in_=sv), writes=["ring3", "ring4"],
                  stream="ringbgl", nobar=True)

        def bg_step():
            k = bg_pos[0]
            if k >= len(bg_list):
                return
            bg_pos[0] += 1
            _, dv, wkey = bg_views(k)
            P.add("dve", lambda e: e.tensor_copy(out=bgb, in_=bgf), reads=["ring3", "ring4"],
                  writes=["ring5"], nobar=True)
            P.add("act", lambda e, dv=dv: e.dma_start(out=dv, in_=bgb), reads=["ring5"],
                  writes=[wkey], stream="ringbgs", nobar=True)
            if k + 1 < len(bg_list):
                bg_load(k + 1)
        P.add("pool", lambda e: e.dma_start(out=wpool_b, in_=wpool_d), writes=["wpool"], stream="cstp")
        for (t, d_, k) in ((convw, convw_d, "convw"), (convb, convb_d, "convb"), (poolsc, poolsc_d, "poolsc"),
                           (normw, normw_d, "normw"), (ln1g, ln1g_d, "ln1g"), (ln1b, ln1b_d, "ln1b"),
                           (ln2g, ln2g_d, "ln2g"), (ln2b, ln2b_d, "ln2b"), (dtb, dtb_d, "dtb"),
                           (aneg, alog_d, "aneg"), (dsk, dsk_d, "dsk")):
            P.add("sp", lambda e, t=t, d_=d_: e.dma_start(out=t, in_=d_), writes=[k, "cst_all"], stream="cst")
        P.add("pool", lambda e: e.memset(ident_f, 1.0), writes=["ident_f"])
        P.add("pool", lambda e: e.affine_select(out=ident_f, in_=ident_f, pattern=[[-1, 128]],
                                                compare_op=ALU.is_equal, fill=0.0, base=0, channel_multiplier=1),
              reads=["ident_f"], writes=["ident_f"])
        P.add("pool", lambda e: e.memset(U_f, 1.0), writes=["U_f"])
        P.add("pool", lambda e: e.affine_select(out=U_f, in_=U_f, pattern=[[1, 128]],
                                                compare_op=ALU.is_ge, fill=0.0, base=0, channel_multiplier=-1),
              reads=["U_f"], writes=["U_f"])
        P.add("pool", lambda e: e.memset(SL_f, 1.0), writes=["SL_f"])
        P.add("pool", lambda e: e.affine_select(out=SL_f, in_=SL_f, pattern=[[-1, 128]],
                                                compare_op=ALU.is_gt, fill=0.0, base=0, channel_multiplier=1),
              reads=["SL_f"], writes=["SL_f"])
        P.add("pool", lambda e: e.memset(ones_f, 1.0), writes=["ones_f"])
        P.add("pool", lambda e: e.memset(ones_b, 1.0), writes=["ones_b"])
        P.add("pool", lambda e: e.memset(negh, -0.5), writes=["negh"])
        P.add("pool", lambda e: e.memset(epsln, LN_EPS), writes=["epsln"])
        P.add("dve", lambda e: e.tensor_copy(out=ident_b, in_=ident_f), reads=["ident_f"], writes=["ident_b"])
        P.add("dve", lambda e: e.tensor_copy(out=U_b, in_=U_f), reads=["U_f"], writes=["U_b"])
        P.add("dve", lambda e: e.tensor_copy(out=SL_r, in_=SL_f), reads=["SL_f"], writes=["SL_r"])
        P.add("act", lambda e: e.activation(out=aneg, in_=aneg, func=AF.Exp), reads=["aneg", "cst_all"], writes=["aneg"])
        P.add("dve", lambda e: e.tensor_scalar(out=aneg, in0=aneg, scalar1=-1.0, scalar2=None, op0=ALU.mult),
              reads=["aneg"], writes=["aneg"])
        for q in range(80):
            P.add("dve", lambda e, q=q: e.tensor_scalar(out=convdiag[:, q, :], in0=ident_b,
                                                        scalar1=convw[:, q:q + 1], scalar2=None, op0=ALU.mult),
                  reads=["ident_b", "convw", "cst_all"], writes=["convdiag"])
        bg_load(0)
        P.barrier(lambda e: e.memset(bart, 0.0))

        def layer_norm(base, g_t, b_t, hT_out=None):
            rb = carve_at(base, [8, 512], BF16)
            sqb = carve_at(base + 8192, [8, 512], BF16)
            mean = carve_at(base + 16384, [512], F32)
            rstd = carve_at(base + 18432, [512], F32)
            m2 = carve_at(base + 20480, [512], F32)
            b1, b2 = next_bank(), next_bank()
            for fc in range(8):
                P.add("act", lambda e, fc=fc: e.activation(out=rb[:, fc, :], in_=resid[:, fc, :], func=AF.Copy),
                      reads=["resid%d" % fc], writes=["rb%d" % fc])
                if fc % 2 == 1:
                    P.add("dve", lambda e, fc=fc: e.tensor_tensor(out=sqb[:, fc, :], in0=resid[:, fc, :],
                                                                  in1=resid[:, fc, :], op=ALU.mult),
                          reads=["resid%d" % fc], writes=["sqb%d" % fc])
                else:
                    P.add("act", lambda e, fc=fc: e.activation(out=sqb[:, fc, :], in_=resid[:, fc, :], func=AF.Square),
                          reads=["resid%d" % fc], writes=["sqb%d" % fc])
            for fc in range(8):
                P.add("pe", lambda e, fc=fc: e.matmul(pst[b1][:], lhsT=ones_b, rhs=rb[:, fc, :],
                                                      start=(fc == 0), stop=(fc == 7)),
                      reads=["rb%d" % fc, "ones_b"], writes=[psk(b1)])
            for fc in range(8):
                P.add("pe", lambda e, fc=fc: e.matmul(pst[b2][:], lhsT=ones_b, rhs=sqb[:, fc, :],
                                                      start=(fc == 0), stop=(fc == 7)),
                      reads=["sqb%d" % fc, "ones_b"], writes=[psk(b2)])
            P.add("act", lambda e: e.activation(out=mean, in_=pst[b1][:], func=AF.Copy, scale=1.0 / D),
                  writes=[psk(b1), "ln_mean"])
            P.add("dve", lambda e: e.tensor_tensor(out=m2, in0=mean, in1=mean, op=ALU.mult),
                  reads=["ln_mean"], writes=["ln_m2"])
            P.add("dve", lambda e: e.scalar_tensor_tensor(out=rstd, in0=pst[b2][:], scalar=1.0 / D, in1=m2,
                                                          op0=ALU.mult, op1=ALU.subtract),
                  reads=["ln_m2"], writes=[psk(b2), "ln_rstd"])
            P.add("act", lambda e: e.activation(out=m2, in_=rstd, func=AF.Ln, bias=epsln[:, 0:1]),
                  reads=["ln_rstd"], writes=["ln_m2"])
            P.add("act", lambda e: e.activation(out=rstd, in_=m2, func=AF.Exp, scale=-0.5),
                  reads=["ln_m2"], writes=["ln_rstd"])
            def gain(fc):
                P.add("dve", lambda e, fc=fc: e.tensor_scalar(out=resid[:, fc, :], in0=resid[:, fc, :],
                                                              scalar1=g_t[:, fc:fc + 1], scalar2=b_t[:, fc:fc + 1],
                                                              op0=ALU.mult, op1=ALU.add),
                      reads=["resid%d" % fc], writes=["resid%d" % fc])
            for fc in range(8):
                k = "resid%d" % fc
                P.add("dve", lambda e, fc=fc: e.tensor_tensor(out=resid[:, fc, :], in0=resid[:, fc, :], in1=mean,
                                                              op=ALU.subtract),
                      reads=[k, "ln_mean"], writes=[k])
                P.add("dve", lambda e, fc=fc: e.tensor_tensor(out=resid[:, fc, :], in0=resid[:, fc, :], in1=rstd,
                                                              op=ALU.mult),
                      reads=[k, "ln_rstd"], writes=[k])
                if hT_out is not None:
                    P.add("act", lambda e, fc=fc: e.activation(out=hT_out[:, fc, :], in_=resid[:, fc, :],
                                                               func=AF.Identity, scale=g_t[:, fc:fc + 1],
                                                               bias=b_t[:, fc:fc + 1]),
                          reads=[k], writes=["hTb%d" % fc])
                else:
                    P.add("act", lambda e, fc=fc: e.activation(out=resid[:, fc, :], in_=resid[:, fc, :],
                                                               func=AF.Identity, scale=g_t[:, fc:fc + 1],
                                                               bias=b_t[:, fc:fc + 1]),
                          reads=[k], writes=[k])
            if hT_out is not None:
                for fc in range(8):
                    gain(fc)

        bar = lambda: P.barrier(lambda e: e.memset(bart, 0.0))

        for ti in range(NT):
            j = ti % (SEQ // TT)
            ring_n[0] = 3 if ti == 0 else NRING
            row0 = ti * TT
            def load_x(r0):
                for st in range(4):
                    P.add("sp", lambda e, st=st, r0=r0: e.dma_start(out=x_tok[:, st, :],
                                                             in_=x_d[r0 + st * 128: r0 + (st + 1) * 128, :]),
                          writes=["x_tok%d" % st], stream="xin%d" % st, nobar=True)
            if ti == 0:
                load_x(0)
            def cast_x(all_act):
                for st in range(4):
                    if all_act or st % 2 == 0:
                        P.add("act", lambda e, st=st: e.activation(out=x_bf[:, st, :], in_=x_tok[:, st, :],
                                                                   func=AF.Copy),
                              reads=["x_tok%d" % st], writes=["x_bf%d" % st])
                    else:
                        P.add("dve", lambda e, st=st: e.tensor_copy(out=x_bf[:, st, :], in_=x_tok[:, st, :]),
                              reads=["x_tok%d" % st], writes=["x_bf%d" % st])
            if ti == 0:
                cast_x(False)
            for fc in range(8):
                b = next_bank()
                pb = pst[b][:].bitcast(BF16)
                for st in range(4):
                    P.add("pe", lambda e, pb=pb, st=st, fc=fc: e.transpose(
                        pb[:, st * 128:(st + 1) * 128], x_bf[:, st, fc * 128:(fc + 1) * 128], ident_b),
                        reads=["x_bf%d" % st, "ident_b"], writes=[psk(b)])
                if fc % 2 == 0:
                    P.add("dve", lambda e, pb=pb, fc=fc: e.tensor_copy(out=xTb[:, fc, :], in_=pb[:, 0:512]),
                          writes=[psk(b), "xTb%d" % fc])
                else:
                    P.add("act", lambda e, pb=pb, fc=fc: e.activation(out=xTb[:, fc, :], in_=pb[:, 0:512],
                                                                      func=AF.Copy),
                          writes=[psk(b), "xTb%d" % fc])

            def resid_transposes(fc):
                b = next_bank()
                for st_ in range(4):
                    P.add("pe", lambda e, b=b, st_=st_, fc=fc: e.transpose(
                        pst[b][:, st_ * 128:(st_ + 1) * 128], x_tok[:, st_, fc * 128:(fc + 1) * 128], ident_f),
                        reads=["x_tok%d" % st_, "ident_f"], writes=[psk(b)])
                P.add("act", lambda e, b=b, fc=fc: e.activation(out=resid[:, fc, :], in_=pst[b][:], func=AF.Copy),
                      writes=[psk(b), "resid%d" % fc])
            if j == 0:
                P.add("pool", lambda e: e.memset(xbchalo, 0.0), writes=["xbchalo"])
                P.add("pool", lambda e: e.memset(poolhalo, 0.0), writes=["poolhalo"])
                P.add("pool", lambda e: e.memset(prevf, 0.0), writes=["prevf"])
                P.add("pool", lambda e: e.memset(prevb, 0.0), writes=["prevb"])
            xk = lambda kc: "xTb%d" % kc
            P.add("pool", lambda e: e.tensor_copy(out=poolA[:, :, 0:15], in_=poolhalo),
                  reads=["poolhalo"], writes=["poolA_h"])

            def evac_pool(c, b):
                P.add("act", lambda e, c=c, b=b: e.activation(out=poolA[:, c, 15:527], in_=pst[b][:], func=AF.Copy),
                      writes=[psk(b), "poolA%d" % c])
            gemm(wq_in, "wq_in", D, 0, 512, xTb, xk, evac_pool, kc_outer_first=True)
            P.add("pool", lambda e: e.tensor_copy(out=poolhalo, in_=poolA[:, :, 512:527]),
                  reads=["poolA%d" % c for c in range(4)] + ["poolA_h"], writes=["poolhalo"])
            for g, w in enumerate(WINS):
                src = poolA[:, g, :]
                bufs = [poolB, poolC]
                lv = int(math.log2(w))
                cur = src
                curk = ["poolA%d" % g, "poolA_h"]
                sh = 1
                for l in range(lv):
                    dst = bufs[l % 2]
                    dk = "poolBC%d" % (l % 2)
                    lo = 2 * sh - 1
                    P.add("pool", lambda e, dst=dst, cur=cur, lo=lo, sh=sh: e.tensor_tensor(
                        out=dst[:, lo:527], in0=cur[:, lo:527], in1=cur[:, lo - sh:527 - sh], op=ALU.add),
                        reads=curk, writes=[dk])
                    cur, curk, sh = dst, [dk], sh * 2
                P.add("dve", lambda e, g=g, cur=cur, w=w: e.scalar_tensor_tensor(
                    out=diff[:, g, :], in0=cur[:, 15:527], scalar=1.0 / w, in1=poolA[:, g, 15:527],
                    op0=ALU.mult, op1=ALU.subtract),
                    reads=curk + ["poolA%d" % g], writes=["diff%d" % g])
                if j == 0:
                    for t in range(w - 1):
                        P.add("dve", lambda e, g=g, cur=cur, t=t: e.scalar_tensor_tensor(
                            out=diff[:, g, t:t + 1], in0=cur[:, 15 + t:16 + t], scalar=1.0 / (t + 1),
                            in1=poolA[:, g, 15 + t:16 + t], op0=ALU.mult, op1=ALU.subtract),
                            reads=curk + ["poolA%d" % g, "diff%d" % g], writes=["diff%d" % g])
            for zb in range(3):
                s = ring_load(wq_in, "wq_in", 0, 512 + zb * 512, 512)
                for st in range(4):
                    b = next_bank()
                    for kc in range(8):
                        P.add("pe", lambda e, b=b, s=s, kc=kc, st=st: e.matmul(
                            pst[b][:], lhsT=xTb[:, kc, st * 128:(st + 1) * 128], rhs=ring[s][:, kc, :],
                            start=(kc == 0), stop=(kc == 7)),
                            reads=["ring%d" % s, xk(kc)], writes=[psk(b)])
                    P.add("act", lambda e, b=b, st=st, zb=zb: e.activation(
                        out=sz[:, st, zb * 512:(zb + 1) * 512], in_=pst[b][:], func=AF.Silu),
                        writes=[psk(b), "sz%d_%d" % (st, zb)])
                if ti == 0:
                    bg_step()
            def pool_matmuls():
                for g in range(4):
                    b = next_bank()
                    P.add("pe", lambda e, g=g, b=b: e.matmul(pst[b][:], lhsT=wpool_b[:, g, :], rhs=diff[:, g, :],
                                                             start=True, stop=True),
                          reads=["diff%d" % g, "wpool"], writes=[psk(b)])
                    P.add("act", lambda e, g=g, b=b: e.activation(out=mixedT[:, g, :], in_=pst[b][:], func=AF.Copy,
                                                                  scale=poolsc[:, g:g + 1]),
                          reads=["poolsc"], writes=[psk(b), "mixedT%d" % g])
            pool_matmuls()
            ectr = [0]

            def evac_xbc(c, b):
                ei = ectr[0] % 2
                ectr[0] += 1
                ek = "ext%d" % ei
                P.add("dve", lambda e, b=b, ei=ei: e.tensor_copy(out=ext[ei][:, 3:515], in_=pst[b][:]),
                      writes=[psk(b), ek])
                P.add("pool", lambda e, c=c, ei=ei: e.tensor_copy(out=ext[ei][:, 0:3], in_=xbchalo[:, c, :]),
                      reads=["xbchalo"], writes=[ek + "h"])
                P.add("pool", lambda e, c=c, ei=ei: e.tensor_copy(out=xbchalo[:, c, :], in_=ext[ei][:, 512:515]),
                      reads=[ek], writes=["xbchalo"])
                b2 = next_bank()
                for k in range(4):
                    P.add("pe", lambda e, b2=b2, c=c, k=k, ei=ei: e.matmul(
                        pst[b2][:], lhsT=convdiag[:, c * 4 + k, :], rhs=ext[ei][:, k:k + 512],
                        start=(k == 0), stop=(k == 3)),
                        reads=[ek, ek + "h", "convdiag"], writes=[psk(b2)])
                P.add("act", lambda e, b2=b2, c=c: e.activation(out=xbcT[:, c, :], in_=pst[b2][:], func=AF.Silu,
                                                                bias=convb[:, c:c + 1]),
                      reads=["convb"], writes=[psk(b2), "xbcT%d" % c])
                if ti == 0 and c % 4 == 3:
                    bg_step()
            gemm(wq_in, "wq_in", D, 2048, 2560, xTb, xk, evac_xbc)
            s = ring_load(wq_in, "wq_in", 0, 4608, 24)
            b = next_bank()
            for st in range(4):
                for kc in range(8):
                    P.add("pe", lambda e, b=b, s=s, kc=kc, st=st: e.matmul(
                        pst[b][:, st * 32:st * 32 + 24], lhsT=xTb[:, kc, st * 128:(st + 1) * 128],
                        rhs=ring[s][:, kc, 0:24], start=(kc == 0), stop=(kc == 7)),
                        reads=["ring%d" % s, xk(kc)], writes=[psk(b)])
            P.add("dve", lambda e, b=b: e.tensor_tensor(
                out=dtmp, in0=pst[b][:, 0:128].rearrange("p (a c) -> p a c", a=4)[:, :, 0:24],
                in1=dtb.unsqueeze(1).to_broadcast([128, 4, 24]), op=ALU.add),
                reads=["dtb"], writes=[psk(b), "dtmp"])
            P.add("act", lambda e: e.activation(out=dexp, in_=dtmp, func=AF.Exp), reads=["dtmp"], writes=["dexp"])
            P.add("act", lambda e: e.activation(out=dt_t, in_=dexp, func=AF.Ln, bias=1.0),
                  reads=["dexp"], writes=["dt_t"])
            P.add("dve", lambda e: e.tensor_tensor(out=a_t, in0=dt_t, in1=aneg.unsqueeze(1).to_broadcast([128, 4, 24]),
                                                   op=ALU.mult),
                  reads=["dt_t", "aneg"], writes=["a_t"])
            if ti == 0:
                dump("d_xbcT", xbcT, [128, 20, 512])
                dump("d_sz", sz, [128, 4, 1536])
                dump("d_dt", dt_t, [128, 4, 24])
                dump("d_pool", mixedT[:, 0:4, :], [128, 4, 512])
                dump("d_xT", xTb, [128, 8, 512])
            bar()
            if ti == 0:
                while bg_pos[0] < len(bg_list):
                    bg_step()
                ring_n[0] = NRING
            bank_pool[0] = [0, 1, 2, 3, 4, 5]
            def stageA(st):
                tsl = slice(st * 128, (st + 1) * 128)
                resid_transposes(2 * st)
                resid_transposes(2 * st + 1)
                for half in range(2):
                    b = next_bank()
                    pb = pst[b][:].bitcast(BF16)
                    for q in range(6):
                        P.add("pe", lambda e, pb=pb, q=q, half=half, tsl=tsl: e.transpose(
                            pb[:, q * 128:(q + 1) * 128], xbcT[:, half * 6 + q, tsl], ident_b),
                            reads=["xbcT%d" % (half * 6 + q), "ident_b"], writes=[psk(b)])
                    hs = slice(12 * half, 12 * half + 12)
                    pv = pb[:, 0:768].rearrange("p (h d) -> p h d", d=64)
                    P.add("dve", lambda e, pv=pv, hs=hs, st=st: e.tensor_tensor(
                        out=xdt[:, hs, :], in0=pv, in1=dt_t[:, st, hs].unsqueeze(2).to_broadcast([128, 12, 64]),
                        op=ALU.mult), reads=["dt_t"], writes=[psk(b), "xdt%d" % half])
                    P.add("dve", lambda e, pv=pv, hs=hs: e.tensor_tensor(
                        out=xsD[:, hs, :], in0=pv, in1=dsk[:, hs].unsqueeze(2).to_broadcast([128, 12, 64]),
                        op=ALU.mult), reads=["dsk"], writes=[psk(b), "xsD%d" % half])
                b = next_bank()
                pb = pst[b][:].bitcast(BF16)
                for g in range(4):
                    P.add("pe", lambda e, pb=pb, g=g, tsl=tsl: e.transpose(
                        pb[:, g * 128:(g + 1) * 128], xbcT[:, 12 + g, tsl], ident_b),
                        reads=["xbcT%d" % (12 + g), "ident_b"], writes=[psk(b)])
                P.add("act", lambda e, pb=pb: e.activation(out=Btok, in_=pb[:, 0:512].rearrange("p (g n) -> p g n", g=4),
                                                           func=AF.Copy),
                      writes=[psk(b), "Btok"])
                b = next_bank()
                for g in range(4):
                    P.add("pe", lambda e, b=b, g=g, tsl=tsl: e.matmul(
                        pst[b][:, g * 128:(g + 1) * 128], lhsT=xbcT[:, 12 + g, tsl], rhs=xbcT[:, 16 + g, tsl],
                        start=(g == 0), stop=(g == 3), skip_group_check=True),
                        reads=["xbcT%d" % (12 + g), "xbcT%d" % (16 + g)], writes=[psk(b)])
                P.add("dve", lambda e, b=b: e.tensor_tensor(
                    out=CBm2[st % 2], in0=pst[b][:].rearrange("p (g n) -> p g n", g=4),
                    in1=U_b.unsqueeze(1).to_broadcast([128, 4, 128]), op=ALU.mult),
                    reads=["U_b"], writes=[psk(b), "CBm%d" % (st % 2)])
                b = next_bank()
                P.add("pe", lambda e, b=b, st=st: e.matmul(pst[b][:, 0:24], lhsT=U_f, rhs=a_t[:, st, :],
                                                           start=True, stop=True, skip_group_check=True),
                      reads=["a_t", "U_f"], writes=[psk(b)])
                P.add("pe", lambda e, b=b, st=st: e.matmul(pst[b][:, 32:56], lhsT=ones_f, rhs=a_t[:, st, :],
                                                           start=False, stop=True, skip_group_check=True),
                      reads=["a_t", "ones_f"], writes=[psk(b)])
                P.add("pe", lambda e, b=b, st=st: e.matmul(pst[b][:, 64:88], lhsT=SL_f, rhs=a_t[:, st, :],
                                                           start=False, stop=True, skip_group_check=True),
                      reads=["a_t", "SL_f"], writes=[psk(b)])
                P.add("act", lambda e, b=b: e.activation(out=ea, in_=pst[b][:, 0:24], func=AF.Exp),
                      writes=[psk(b), "ea"])
                P.add("act", lambda e, b=b: e.activation(out=cd, in_=pst[b][:, 32:56], func=AF.Exp),
                      writes=[psk(b), "cd"])
                P.add("act", lambda e, b=b: e.activation(out=dte, in_=pst[b][:, 64:88], func=AF.Exp),
                      writes=[psk(b), "dte"])
            stageA(0)
            for st in range(4):
                tsl = slice(st * 128, (st + 1) * 128)

                def emit_xdec():
                    P.add("pool", lambda e: e.tensor_tensor(out=xdec, in0=xdt,
                                                            in1=dte.unsqueeze(2).to_broadcast([128, 24, 64]),
                                                            op=ALU.mult),
                          reads=["xdt0", "xdt1", "dte"], writes=["xdec"])

                def tri_front(stx, q3):
                    g = q3 // 2
                    ri, mi = (stx * 8 + q3) % 2, (stx * 8 + q3) % 3
                    hs = slice(3 * q3, 3 * q3 + 3)
                    CBm = CBm2[stx % 2]
                    P.add("pool", lambda e, ri=ri, hs=hs, stx=stx: e.tensor_tensor(
                        out=R3[ri], in0=a_t[:, stx, hs].unsqueeze(2).to_broadcast([128, 3, 128]),
                        in1=U_f.unsqueeze(1).to_broadcast([128, 3, 128]), op=ALU.mult),
                        reads=["a_t", "U_f"], writes=["R3_%d" % ri])
                    b = next_bank()
                    P.add("pe", lambda e, b=b, ri=ri: e.matmul(
                        pst[b][:, 0:384], lhsT=SL_r, rhs=R3[ri][:].rearrange("p h l -> p (h l)"),
                        start=True, stop=True),
                        reads=["R3_%d" % ri, "SL_r"], writes=[psk(b)])
                    P.add("act", lambda e, b=b, ri=ri: e.activation(
                        out=L3[ri][:].rearrange("p h l -> p (h l)"), in_=pst[b][:, 0:384], func=AF.Exp),
                        writes=[psk(b), "L3_%d" % ri])
                    P.add("dve", lambda e, ri=ri, mi=mi, g=g, CBm=CBm: e.tensor_tensor(
                        out=M3[mi], in0=L3[ri], in1=CBm[:, g, :].unsqueeze(1).to_broadcast([128, 3, 128]),
                        op=ALU.mult),
                        reads=["L3_%d" % ri, "CBm%d" % (stx % 2)], writes=["M3_%d" % mi])

                def tri_back(q3):
                    g = q3 // 2
                    mi = (st * 8 + q3) % 3
                    yb = 6 + (g % 2)
                    for hh in range(3):
                        h = 3 * q3 + hh
                        hl = h % 6
                        P.add("pe", lambda e, yb=yb, mi=mi, hh=hh, h=h, hl=hl: e.matmul(
                            pst[yb][:, hl * 64:(hl + 1) * 64], lhsT=M3[mi][:, hh, :], rhs=xdt[:, h, :],
                            start=(hl == 0), stop=False, skip_group_check=True),
                            reads=["M3_%d" % mi, "xdt%d" % (h // 12)], writes=[psk(yb)])
                    if q3 % 2 == 1:
                        P.add("pe", lambda e, yb=yb, g=g: e.matmul(
                            pst[yb][:, 0:384], lhsT=ident_b,
                            rhs=xsD[:, 6 * g:6 * g + 6, :].rearrange("p h d -> p (h d)"),
                            start=False, stop=True, skip_group_check=True),
                            reads=["xsD%d" % (g // 2), "ident_b"], writes=[psk(yb)])

                def group_post(g):
                    hs6 = slice(6 * g, 6 * g + 6)
                    ti1 = g % 2
                    yb = 6 + (g % 2)
                    bo = next_bank()
                    P.add("pe", lambda e, bo=bo, g=g, tsl=tsl: e.matmul(
                        pst[bo][:, 0:384], lhsT=xbcT[:, 16 + g, tsl], rhs=prevb[:, g, :], start=True, stop=True),
                        reads=["xbcT%d" % (16 + g), "prevb%d" % g], writes=[psk(bo)])
                    bs = next_bank()
                    P.add("pe", lambda e, bs=bs, g=g, hs6=hs6: e.matmul(
                        pst[bs][:, 0:384], lhsT=Btok[:, g, :], rhs=xdec[:, hs6, :].rearrange("p h d -> p (h d)"),
                        start=True, stop=True),
                        reads=["Btok", "xdec"], writes=[psk(bs)])
                    P.add("dve", lambda e, bo=bo, ti1=ti1, hs6=hs6: e.tensor_tensor(
                        out=t1[ti1], in0=pst[bo][:, 0:384].rearrange("p (h d) -> p h d", d=64),
                        in1=ea[:, hs6].unsqueeze(2).to_broadcast([128, 6, 64]), op=ALU.mult),
                        reads=["ea"], writes=[psk(bo), "t1_%d" % ti1])
                    P.add("dve", lambda e, yb=yb, ti1=ti1: e.tensor_tensor(
                        out=t1[ti1], in0=pst[yb][:, 0:384].rearrange("p (h d) -> p h d", d=64), in1=t1[ti1],
                        op=ALU.add),
                        reads=["t1_%d" % ti1], writes=[psk(yb), "t1_%d" % ti1])
                    P.add("dve", lambda e, g=g, ti1=ti1, st=st: e.tensor_tensor(
                        out=vv[:, g, :], in0=t1[ti1][:].rearrange("p h d -> p (h d)"),
                        in1=sz[:, st, g * 384:(g + 1) * 384], op=ALU.mult),
                        reads=["t1_%d" % ti1] + ["sz%d_%d" % (st, zb) for zb in range(3)], writes=["vv%d" % g])
                    P.add("pool", lambda e, g=g, hs6=hs6, ti1=ti1: e.tensor_tensor(
                        out=t2[ti1], in0=prevf[:, g, :].rearrange("p (h d) -> p h d", d=64),
                        in1=cd[:, hs6].unsqueeze(2).to_broadcast([128, 6, 64]), op=ALU.mult),
                        reads=["prevf%d" % g, "cd"], writes=["t2_%d" % ti1])
                    P.add("dve", lambda e, bs=bs, g=g, ti1=ti1: e.tensor_tensor(
                        out=prevf[:, g, :], in0=pst[bs][:, 0:384], in1=t2[ti1][:].rearrange("p h d -> p (h d)"),
                        op=ALU.add),
                        reads=["t2_%d" % ti1], writes=[psk(bs), "prevf%d" % g])
                    if g > 0:
                        group_act(g - 1)

                def group_act(g):
                    P.add("act", lambda e, g=g: e.activation(out=junk, in_=vv[:, g, :], func=AF.Square,
                                                             accum_out=ss[:, g:g + 1]),
                          reads=["vv%d" % g], writes=["junk", "ss%d" % g])
                    P.add("act", lambda e, g=g: e.activation(out=prevb[:, g, :], in_=prevf[:, g, :], func=AF.Copy),
                          reads=["prevf%d" % g], writes=["prevb%d" % g])

                if st == 0:
                    tri_front(0, 0)
                    tri_front(0, 1)
                for q3 in range(8):
                    if q3 + 2 < 8:
                        tri_front(st, q3 + 2)
                    if q3 == 0:
                        emit_xdec()
                    tri_back(q3)
                    if q3 % 2 == 1:
                        group_post(q3 // 2)
                group_act(3)
                ssk = ["ss%d" % g for g in range(4)]
                P.add("dve", lambda e: e.tensor_scalar(out=ms, in0=ss, scalar1=1.0 / 384, scalar2=RMS_EPS,
                                                       op0=ALU.mult, op1=ALU.add), reads=ssk, writes=["ms"])
                P.add("act", lambda e: e.activation(out=lnms, in_=ms, func=AF.Ln), reads=["ms"], writes=["lnms"])
                P.add("act", lambda e: e.activation(out=rstd4, in_=lnms, func=AF.Exp, scale=-0.5),
                      reads=["lnms"], writes=["rstd4"])
                for g in range(4):
                    P.add("act", lambda e, g=g: e.activation(out=vn[:, g, :], in_=vv[:, g, :], func=AF.Copy,
                                                             scale=rstd4[:, g:g + 1]),
                          reads=["rstd4", "vv%d" % g], writes=["vn%d" % g])
                if st + 1 < 4:
                    stageA(st + 1)
                    tri_front(st + 1, 0)
                    tri_front(st + 1, 1)
                if ti == 0 and st == 0:
                    dump("d_vv0", vv, [128, 4, 384])
                    dump("d_ss", ss, [128, 4])
                    dump("d_rstd4", rstd4, [128, 4])
                    dump("d_ms", ms, [128, 4])
                    dump("d_lnms", lnms, [128, 4])
                    dump("d_vn", vn, [128, 4, 384])
                vnf = vn[:].rearrange("p g d -> p (g d)")
                for half in range(2):
                    b = next_bank()
                    pb = pst[b][:].bitcast(BF16)
                    for q in range(6):
                        c = half * 6 + q
                        P.add("pe", lambda e, pb=pb, q=q, c=c: e.transpose(
                            pb[:, q * 128:(q + 1) * 128], vnf[:, c * 128:(c + 1) * 128], ident_b),
                            reads=["vn%d" % (c // 3), "ident_b"], writes=[psk(b)])
                    P.add("dve", lambda e, pb=pb, half=half, tsl=tsl: e.tensor_tensor(
                        out=mixedT[:, 4 + 6 * half:10 + 6 * half, tsl],
                        in0=pb[:, 0:768].rearrange("p (c t) -> p c t", c=6),
                        in1=normw[:, 6 * half:6 * half + 6].unsqueeze(2).to_broadcast([128, 6, 128]), op=ALU.mult),
                        reads=["normw"], writes=[psk(b)] + ["mixedT%d" % (4 + 6 * half + q) for q in range(6)])
            if ti == 0:
                dump("d_mixedT", mixedT, [128, 16, 512])
                dump("d_vv", vv, [128, 4, 384])
                dump("d_prevf", prevf, [128, 4, 384])
            if ti + 1 < NT:
                load_x((ti + 1) * TT)
            bank_pool[0] = list(range(8))
            def evac_res(c, b):
                P.add("dve", lambda e, c=c, b=b: e.scalar_tensor_tensor(
                    out=resid[:, c, :], in0=resid[:, c, :], scalar=ALPHA, in1=pst[b][:],
                    op0=ALU.mult, op1=ALU.add), reads=["resid%d" % c], writes=[psk(b), "resid%d" % c])
            gemm(wq_out, "wq_out", 2048, 0, D, mixedT, lambda kc: "mixedT%d" % kc, evac_res)
            layer_norm(LN3, ln1g, ln1b, hT_out=hTb)
            if ti == 0:
                dump("d_h", resid, [128, 8, 512])
            rctr = [0]

            def evac_ff1(c, b):
                ri = rctr[0] % 2
                rctr[0] += 1
                P.add("act", lambda e, b=b, ri=ri: e.activation(out=rl[ri], in_=pst[b][:], func=AF.Relu),
                      writes=[psk(b), "rl%d" % ri])
                P.add("dve", lambda e, c=c, ri=ri: e.tensor_tensor(out=hid[:, c, :], in0=rl[ri], in1=rl[ri],
                                                                   op=ALU.mult),
                      reads=["rl%d" % ri], writes=["hid%d" % c])
            gemm(wq_ff1, "wq_ff1", D, 0, DFF, hTb, lambda kc: "hTb%d" % kc, evac_ff1, kc_outer_first=True)
            gemm(wq_ff2, "wq_ff2", DFF, 0, D, hid, lambda kc: "hid%d" % kc, evac_res)
            if ti + 1 < NT:
                cast_x(True)
            layer_norm(LN4, ln2g, ln2b)
            for st in range(4):
                oi = st % 2
                for hf in range(2):
                    b = next_bank()
                    for q in range(4):
                        fc = hf * 4 + q
                        P.add("pe", lambda e, b=b, q=q, fc=fc, st=st: e.transpose(
                            pst[b][:, q * 128:(q + 1) * 128], resid[:, fc, st * 128:(st + 1) * 128], ident_f),
                            reads=["resid%d" % fc, "ident_f"], writes=[psk(b)])
                    if hf == 0:
                        P.add("act", lambda e, b=b, oi=oi: e.activation(out=ostage[oi][:, 0:512], in_=pst[b][:],
                                                                        func=AF.Copy),
                              writes=[psk(b), "ost%d_0" % oi])
                    else:
                        P.add("dve", lambda e, b=b, oi=oi: e.tensor_copy(out=ostage[oi][:, 512:1024], in_=pst[b][:]),
                              writes=[psk(b), "ost%d_1" % oi])
                P.add("act", lambda e, oi=oi, st=st, row0=row0: e.dma_start(
                    out=out_d[row0 + st * 128: row0 + (st + 1) * 128, :], in_=ostage[oi]),
                    reads=["ost%d_0" % oi, "ost%d_1" % oi], stream="outst%d" % oi)
            bar()
        P.emit(ctx)
        build_nc.stats = P.stats
    return nc


_NC = None


def kernel(x, w_in, w_pool, pool_scale, conv_w, conv_b, dt_bias, a_log, d_skip, ssd_norm_w, w_out,
           ln1_g, ln1_b, w_ff1, w_ff2, ln2_g, ln2_b):
    global _NC
    if _NC is None:
        _NC = build_nc()
    nc = _NC
    f = lambda a: np.ascontiguousarray(np.asarray(a, dtype=np.float32))
    x = f(x).reshape(16 * SEQ, D)
    rep = lambda v: f(np.broadcast_to(f(v).reshape(1, -1), (128, f(v).size)))
    colm = lambda v, n: f(f(v).reshape(n, 128).T)
    shared = {
        "w_in": f(w_in[0]), "w_out": f(w_out[0]), "w_ff1": f(w_ff1[0]), "w_ff2": f(w_ff2[0]),
        "w_pool_l": f(np.transpose(f(w_pool[0]), (1, 0, 2))),
        "conv_w_l": f(np.transpose(f(conv_w[0]).reshape(4, 20, 128), (2, 1, 0)).reshape(128, 80)),
        "conv_b_l": colm(conv_b[0], 20),
        "pool_scale_l": colm(pool_scale[0], 4),
        "ssd_norm_w_l": colm(ssd_norm_w[0], 12),
        "ln1_g_l": colm(ln1_g[0], 8), "ln1_b_l": colm(ln1_b[0], 8),
        "ln2_g_l": colm(ln2_g[0], 8), "ln2_b_l": colm(ln2_b[0], 8),
        "dt_bias_r": rep(dt_bias[0]), "a_log_r": rep(a_log[0]), "d_skip_r": rep(d_skip[0]),
    }
    in_maps = []
    for c in range(NCORES):
        m = dict(shared)
        m["x"] = np.ascontiguousarray(x[c * TOK:(c + 1) * TOK])
        in_maps.append(m)
    res = run_bass_kernel_spmd(nc, in_maps, core_ids=list(range(NCORES)))
    out = np.concatenate([np.asarray(r["out"], dtype=np.float32) for r in res.results], axis=0)
    return out.reshape(16, SEQ, D)
```

```python
import math
from contextlib import ExitStack
import numpy as np
import concourse.bass as bass
import concourse.mybir as mybir
from concourse.bass_utils import run_bass_kernel_spmd

F32 = mybir.dt.float32
BF16 = mybir.dt.bfloat16
F32R = mybir.dt.float32r
AF = mybir.ActivationFunctionType
ALU = mybir.AluOpType

NCORES = 8
D = 1024
TOK = 4096
TT = 512
NT = TOK // TT
SEQ = 2048
DIN = 4632
DFF = 4096
ALPHA = 2.0 ** 0.25
LN_EPS = 1e-5
RMS_EPS = 1e-5
WINS = (2, 4, 8, 16)

ENGS = ("pe", "act", "dve", "pool", "sp")


class Prog:
    def __init__(self, nc):
        self.nc = nc
        self.ops = []
        self.lastw = {}
        self.readers = {}
        self.streams = {}
        self.last_eng = {}
        self.last_stream = {}
        self.bar = None

    def _ekey(self, i):
        o = self.ops[i]
        return ("s", o["stream"]) if o["stream"] is not None else ("e", o["eng"])

    def add(self, eng, fn, reads=(), writes=(), stream=None, extra=(), nobar=False):
        i = len(self.ops)
        deps = set(extra)
        raw = set()
        for k in reads:
            w = self.lastw.get(k)
            if w is not None:
                deps.add(w)
                raw.add(w)
        for k in writes:
            w = self.lastw.get(k)
            if w is not None:
                deps.add(w)
            deps.update(self.readers.get(k, {}).values())
        if self.bar is not None and not nobar:
            deps.add(self.bar)
        fdeps = set()
        for d in deps:
            od = self.ops[d]
            if od["stream"] is None and stream is None and od["eng"] == eng:
                if eng == "pe" or d not in raw:
                    continue
            fdeps.add(d)
        sval = None
        if stream is not None:
            self.streams[stream] = self.streams.get(stream, 0) + 1
            sval = 16 * self.streams[stream]
        self.ops.append(dict(eng=eng, fn=fn, deps=fdeps, stream=stream, sval=sval, hasdep=False))
        for d in fdeps:
            self.ops[d]["hasdep"] = True
        ek = self._ekey(i)
        for k in reads:
            self.readers.setdefault(k, {})[ek] = i
        for k in writes:
            self.lastw[k] = i
            self.readers[k] = {}
        if stream is None:
            self.last_eng[eng] = i
        else:
            self.last_stream[stream] = i
        return i

    def barrier(self, fn):
        front = set(self.last_eng.values()) | set(
            v for k, v in self.last_stream.items() if not (k.startswith("wcast_") or k.startswith("ring")))
        self.bar = None
        b = self.add("pool", fn, extra=front)
        self.bar = b
        keep = lambda k: k.startswith("ring") or k.startswith("wq_") or k.startswith("x_tok")
        self.lastw = {k: v for k, v in self.lastw.items() if keep(k)}
        self.readers = {k: v for k, v in self.readers.items() if keep(k)}

    def emit(self, ctx):
        nc = self.nc
        ops = self.ops
        esem = {e: ctx.enter_context(nc.semaphore("e_" + e)) for e in ENGS}
        ssem = {s: ctx.enter_context(nc.semaphore("s_" + s)) for s in self.streams}
        cnt = {e: 0 for e in ENGS}
        for o in ops:
            if o["stream"] is None:
                if o["hasdep"]:
                    cnt[o["eng"]] += 1
                    o["tick"] = cnt[o["eng"]]
                else:
                    o["tick"] = None
        engvc = {e: {} for e in ENGS}
        for o in ops:
            vc = engvc[o["eng"]]
            wm = {}
            for d in sorted(o["deps"]):
                od = ops[d]
                if od["stream"] is not None:
                    key, val = ("s", od["stream"]), od["sval"]
                else:
                    key, val = ("e", od["eng"]), od["tick"]
                if vc.get(key, 0) < val:
                    wm[key] = max(wm.get(key, 0), val)
                    vc[key] = val
                for k2, v2 in od["vc"].items():
                    if vc.get(k2, 0) < v2:
                        vc[k2] = v2
            o["waits"] = wm
            snap = dict(vc)
            if o["stream"] is None and o["tick"] is not None:
                snap[("e", o["eng"])] = o["tick"]
            o["vc"] = snap
        final = [(("s", s), 16 * c) for s, c in self.streams.items()]
        self.stats = dict(nops=len(ops), nwaits=sum(len(o["waits"]) for o in ops), ticks=dict(cnt),
                          per_eng={e: sum(1 for o in ops if o["eng"] == e) for e in ENGS})

        def sem_of(key):
            return ssem[key[1]] if key[0] == "s" else esem[key[1]]

        def run(engname):
            def body(eng):
                for o in ops:
                    if o["eng"] != engname:
                        continue
                    for k, v in o["waits"].items():
                        eng.wait_ge(sem_of(k), v)
                    ins = o["fn"](eng)
                    if o["stream"] is not None:
                        ins.then_inc(ssem[o["stream"]], 16)
                    elif o["tick"] is not None:
                        ins.then_inc(esem[engname], 1)
                if engname == "sp":
                    for k, v in final:
                        eng.wait_ge(sem_of(k), v)
            return body

        with nc.Block() as block:
            block.tensor(run("pe"))
            block.scalar(run("act"))
            block.vector(run("dve"))
            block.gpsimd(run("pool"))
            block.sync(run("sp"))


DEBUG = False


def build_nc():
    nc = bass.Bass("TRN2", target_bir_lowering=False)
    dbg = {}

    def dbg_out(name, shape):
        dbg[name] = nc.dram_tensor(name, list(shape), F32, kind="ExternalOutput").ap()
        return dbg[name]

    def din(name, shape):
        return nc.dram_tensor(name, list(shape), F32, kind="ExternalInput").ap()

    x_d = din("x", [TOK, D])
    w_in_d = din("w_in", [D, DIN])
    w_out_d = din("w_out", [2048, D])
    w_ff1_d = din("w_ff1", [D, DFF])
    w_ff2_d = din("w_ff2", [DFF, D])
    wpool_d = din("w_pool_l", [128, 4, 128])
    convw_d = din("conv_w_l", [128, 80])
    convb_d = din("conv_b_l", [128, 20])
    poolsc_d = din("pool_scale_l", [128, 4])
    normw_d = din("ssd_norm_w_l", [128, 12])
    ln1g_d = din("ln1_g_l", [128, 8])
    ln1b_d = din("ln1_b_l", [128, 8])
    ln2g_d = din("ln2_g_l", [128, 8])
    ln2b_d = din("ln2_b_l", [128, 8])
    dtb_d = din("dt_bias_r", [128, 24])
    alog_d = din("a_log_r", [128, 24])
    dsk_d = din("d_skip_r", [128, 24])
    out_d = nc.dram_tensor("out", [TOK, D], F32, kind="ExternalOutput").ap()
    wq_in = nc.dram_tensor("wq_in", [D, DIN], BF16, kind="Internal").ap()
    wq_out = nc.dram_tensor("wq_out", [2048, D], BF16, kind="Internal").ap()
    wq_ff1 = nc.dram_tensor("wq_ff1", [D, DFF], BF16, kind="Internal").ap()
    wq_ff2 = nc.dram_tensor("wq_ff2", [DFF, D], BF16, kind="Internal").ap()

    with ExitStack() as ctx:
        TOTAL = 206848
        big = ctx.enter_context(nc.sbuf_tensor("big", [128, TOTAL // 2], BF16))
        pst = [ctx.enter_context(nc.psum_tensor("ps%d" % i, [128, 512], F32)) for i in range(8)]
        state = dict(off=0)

        def carve_at(off, free, dt):
            nb = int(np.prod(free)) * (2 if dt == BF16 else 4)
            assert off % 4 == 0 and off + nb <= TOTAL, (off, nb)
            v = big[:, off // 2:(off + nb) // 2]
            if dt != BF16:
                v = v.bitcast(dt)
            if len(free) == 2:
                v = v.rearrange("p (a b) -> p a b", a=free[0])
            return v

        def carve(free, dt):
            nb = int(np.prod(free)) * (2 if dt == BF16 else 4)
            nb = (nb + 63) // 64 * 64
            v = carve_at(state["off"], free, dt)
            state["off"] += nb
            return v

        convdiag = carve([80, 128], BF16)
        NRING = 6
        ring_off = state["off"]
        ring = [carve([8, 512], BF16) for _ in range(NRING)]
        bgf = carve_at(ring_off + 3 * 8192, [8, 512], F32)
        bgb = ring[5]
        x_tok = carve([4, 1024], F32)
        resid = carve([8, 512], F32)
        ident_b = carve([128], BF16)
        ident_f = carve([128], F32)
        U_f = carve([128], F32)
        ones_f = carve([128], F32)
        SL_f = carve([128], F32)
        SL_r = ctx.enter_context(nc.sbuf_tensor("SL_r", [128, 128], F32R))[:]
        U_b = carve([128], BF16)
        ones_b = carve([128], BF16)
        negh = carve([512], F32)
        epsln = carve([16], F32)
        convw = carve([80], F32)
        convb = carve([20], F32)
        poolsc = carve([4], F32)
        normw = carve([12], F32)
        ln1g = carve([8], F32)
        ln1b = carve([8], F32)
        ln2g = carve([8], F32)
        ln2b = carve([8], F32)
        dtb = carve([24], F32)
        aneg = carve([24], F32)
        dsk = carve([24], F32)
        wpool_b = carve([4, 128], BF16)
        xbchalo = carve([20, 3], BF16)
        poolhalo = carve([4, 15], F32)
        prevf = carve([4, 384], F32)
        prevb = carve([4, 384], BF16)
        bart = carve([16], F32)
        S = state["off"]
        SCR = TOTAL - S
        assert SCR >= 81920, SCR

        xbcT = carve_at(S + 0, [20, 512], BF16)
        sz = carve_at(S + 20480, [4, 1536], BF16)
        mixedT = carve_at(S + 32768, [16, 512], BF16)
        dt_t = carve_at(S + 49152, [4, 24], F32)
        a_t = carve_at(S + 49152 + 384, [4, 24], F32)
        T0 = S + 49920
        xTb = carve_at(T0, [8, 512], BF16)
        o1 = T0 + 8192
        poolA = carve_at(o1, [4, 527], F32); o1 += 8448
        poolB = carve_at(o1, [527], F32); o1 += 2112
        poolC = carve_at(o1, [527], F32); o1 += 2112
        diff = carve_at(o1, [4, 512], BF16); o1 += 4096
        ext = [carve_at(o1, [516], BF16), carve_at(o1 + 1040, [516], BF16)]; o1 += 2080
        dtmp = carve_at(o1, [4, 24], F32); o1 += 384
        dexp = carve_at(o1, [4, 24], F32); o1 += 384
        x_bf = carve_at(o1, [4, 1024], BF16); o1 += 8192
        assert o1 <= TOTAL, (o1, TOTAL)
        o2 = T0
        xdt = carve_at(o2, [24, 64], BF16); o2 += 3072
        xsD = carve_at(o2, [24, 64], BF16); o2 += 3072
        xdec = carve_at(o2, [24, 64], BF16); o2 += 3072
        Btok = carve_at(o2, [4, 128], BF16); o2 += 1024
        CBm2 = [carve_at(o2, [4, 128], BF16), carve_at(o2 + 1024, [4, 128], BF16)]; o2 += 2048
        R3 = [ctx.enter_context(nc.sbuf_tensor("R3_%d" % i, [128, 384], F32R))[:].rearrange("p (h l) -> p h l", h=3)
              for i in range(2)]
        L3 = [carve_at(o2, [3, 128], BF16), carve_at(o2 + 768, [3, 128], BF16)]; o2 += 1536
        M3 = [carve_at(o2 + 768 * m, [3, 128], BF16) for m in range(3)]; o2 += 2304
        t1 = [carve_at(o2, [6, 64], F32), carve_at(o2 + 1536, [6, 64], F32)]; o2 += 3072
        t2 = [carve_at(o2, [6, 64], F32), carve_at(o2 + 1536, [6, 64], F32)]; o2 += 3072
        vv = carve_at(o2, [4, 384], F32); o2 += 6144
        vn = carve_at(o2, [4, 384], BF16); o2 += 3072
        junk = carve_at(o2, [384], F32); o2 += 1536
        acs = carve_at(o2, [24], F32); o2 += 128
        ea = carve_at(o2, [24], F32); o2 += 128
        cd = carve_at(o2, [24], F32); o2 += 128
        dte = carve_at(o2, [24], F32); o2 += 128
        tmpd = carve_at(o2, [24], F32); o2 += 128
        ss = carve_at(o2, [4], F32); o2 += 64
        ms = carve_at(o2, [4], F32); o2 += 64
        rstd4 = carve_at(o2, [4], F32); o2 += 64
        lnms = carve_at(o2, [4], F32); o2 += 64
        assert o2 <= TOTAL
        hTb = carve_at(S + 0, [8, 512], BF16)
        hid = carve_at(S + 8192, [32, 512], BF16)
        rl = [carve_at(S + 40960, [512], BF16), carve_at(S + 41984, [512], BF16)]
        ostage = [carve_at(S + 43008, [1024], F32), carve_at(S + 47104, [1024], F32)]
        LN3 = T0
        LN4 = S + 51200
        assert LN4 + 22528 <= TOTAL and LN3 + 22528 <= TOTAL

        P = Prog(nc)
        bank_ctr = [0]

        def dump(name, ap, shape):
            if not DEBUG:
                return
            d = dbg_out(name, shape)
            front = set(P.last_eng.values()) | set(v for k, v in P.last_stream.items() if not k.startswith("ring"))
            i = P.add("pool", lambda e, d=d, ap=ap: e.dma_start(out=d, in_=ap), stream="dbg", extra=front)
            P.add("pool", lambda e: e.memset(bart, 0.0), extra=[i])
            P.barrier(lambda e: e.memset(bart, 0.0))

        bank_pool = [list(range(8))]

        def next_bank():
            pool_ = bank_pool[0]
            b = pool_[bank_ctr[0] % len(pool_)]
            bank_ctr[0] += 1
            return b

        def psk(b):
            return "ps%d" % b

        ring_ctr = [0]
        ring_n = [3]

        def ring_load(wq, wkey, kq, c0, n):
            s = ring_ctr[0] % ring_n[0]
            ring_ctr[0] += 1
            wkey = "%s_b%d_%d" % (wkey, kq, c0 // 512)

            src = wq.rearrange("(kc p) n -> p kc n", p=128)[:, kq * 8:(kq + 1) * 8, c0:c0 + n]
            P.add("sp", lambda e, s=s, src=src, n=n: e.dma_start(out=ring[s][:, :, 0:n], in_=src),
                  reads=[wkey], writes=["ring%d" % s], stream="ring%d" % s, nobar=True)
            return s

        def gemm(wq, wkey, K, c0, ncols, rhsT, rkeys, evac, kc_outer_first=False):
            nkq = K // 1024
            for nb in range((ncols + 511) // 512):
                n = min(512, ncols - nb * 512)
                nch = n // 128
                banks = [next_bank() for _ in range(nch)]
                for kq in range(nkq):
                    s = ring_load(wq, wkey, kq, c0 + nb * 512, n)
                    order = [(fcl, kc) for fcl in range(nch) for kc in range(8)]
                    if kc_outer_first and nb == 0:
                        order = [(fcl, kc) for kc in range(8) for fcl in range(nch)]
                    for fcl, kc in order:
                        if True:
                            P.add("pe", lambda e, b=banks[fcl], s=s, kc=kc, fcl=fcl, kq=kq:
                                  e.matmul(pst[b][:], lhsT=ring[s][:, kc, fcl * 128:(fcl + 1) * 128],
                                           rhs=rhsT[:, kq * 8 + kc, :],
                                           start=(kq == 0 and kc == 0), stop=(kq == nkq - 1 and kc == 7)),
                                  reads=["ring%d" % s, rkeys(kq * 8 + kc)], writes=[psk(banks[fcl])])
                for fcl in range(nch):
                    evac(nb * 4 + fcl, banks[fcl])

        def cast_w(src, dst, rows, key, extra=()):
            for r in range(rows // 128):
                P.add("pool", lambda e, r=r: e.dma_start(out=dst[r * 128:(r + 1) * 128, :],
                                                         in_=src[r * 128:(r + 1) * 128, :]),
                      writes=[key], stream="wcast_" + key, nobar=True, extra=extra)
        stg_f = [carve_at(S + i * 16384, [8, 512], F32) for i in range(3)]
        stg_b = [carve_at(S + 49152 + i * 8192, [8, 512], BF16) for i in range(3)]
        sctr = [0]

        def stage_cast(src, dst, K, ncols, kname):
            for kq in range(K // 1024):
                for blk in range((ncols + 511) // 512):
                    n = min(512, ncols - blk * 512)
                    i = sctr[0] % 3
                    sctr[0] += 1
                    sv = src[kq * 1024:(kq + 1) * 1024, :].rearrange("(kc p) n -> p kc n", p=128)[
                        :, :, blk * 512:blk * 512 + n]
                    dv = dst[kq * 1024:(kq + 1) * 1024, :].rearrange("(kc p) n -> p kc n", p=128)[
                        :, :, blk * 512:blk * 512 + n]
                    P.add("sp", lambda e, i=i, sv=sv, n=n: e.dma_start(out=stg_f[i][:, :, 0:n], in_=sv),
                          writes=["stgf%d" % i], stream="stgl%d" % i)
                    ce = ("dve", "act")[sctr[0] % 2]
                    if ce == "dve":
                        P.add("dve", lambda e, i=i, n=n: e.tensor_copy(out=stg_b[i][:, :, 0:n],
                                                                       in_=stg_f[i][:, :, 0:n]),
                              reads=["stgf%d" % i], writes=["stgb%d" % i])
                    else:
                        P.add("act", lambda e, i=i, n=n: e.activation(out=stg_b[i][:, :, 0:n],
                                                                      in_=stg_f[i][:, :, 0:n], func=AF.Copy),
                              reads=["stgf%d" % i], writes=["stgb%d" % i])
                    P.add("act", lambda e, i=i, dv=dv, n=n: e.dma_start(out=dv, in_=stg_b[i][:, :, 0:n]),
                          reads=["stgb%d" % i], writes=["%s_b%d_%d" % (kname, kq, blk)], stream="stgs%d" % i)
        stage_cast(w_in_d, wq_in, D, DIN, "wq_in")
        stage_cast(w_out_d, wq_out, 2048, D, "wq_out")
        stage_cast(w_ff1_d, wq_ff1, D, DFF, "wq_ff1")
        bg_list = [(w_ff2_d, wq_ff2, "wq_ff2", kq, blk) for blk in range(2) for kq in range(4)]
        bg_pos = [0]

        def bg_views(k):
            src, dst, kname, kq, blk = bg_list[k]
            sv = src[kq * 1024:(kq + 1) * 1024, :].rearrange("(kc p) n -> p kc n", p=128)[:, :, blk * 512:(blk + 1) * 512]
            dv = dst[kq * 1024:(kq + 1) * 1024, :].rearrange("(kc p) n -> p kc n", p=128)[:, :, blk * 512:(blk + 1) * 512]
            return sv, dv, "%s_b%d_%d" % (kname, kq, blk)

        def bg_load(k):
            sv, _, _ = bg_views(k)
            P.add("act", lambda e, sv=sv: e.dma_start(out=bgf, in_=sv), writes=["ring3", "ring4"],
                  stream="ringbgl", nobar=True)

        def bg_step():
            k = bg_pos[0]
            if k >= len(bg_list):
                return
            bg_pos[0] += 1
            _, dv, wkey = bg_views(k)
            P.add("dve", lambda e: e.tensor_copy(out=bgb, in_=bgf), reads=["ring3", "ring4"],
                  writes=["ring5"], nobar=True)
            P.add("act", lambda e, dv=dv: e.dma_start(out=dv, in_=bgb), reads=["ring5"],
                  writes=[wkey], stream="ringbgs", nobar=True)
            if k + 1 < len(bg_list):
                bg_load(k + 1)
        P.add("pool", lambda e: e.dma_start(out=wpool_b, in_=wpool_d), writes=["wpool"], stream="cstp")
        for (t, d_, k) in ((convw, convw_d, "convw"), (convb, convb_d, "convb"), (poolsc, poolsc_d, "poolsc"),
                           (normw, normw_d, "normw"), (ln1g, ln1g_d, "ln1g"), (ln1b, ln1b_d, "ln1b"),
                           (ln2g, ln2g_d, "ln2g"), (ln2b, ln2b_d, "ln2b"), (dtb, dtb_d, "dtb"),
                           (aneg, alog_d, "aneg"), (dsk, dsk_d, "dsk")):
            P.add("sp", lambda e, t=t, d_=d_: e.dma_start(out=t, in_=d_), writes=[k, "cst_all"], stream="cst")
        P.add("pool", lambda e: e.memset(ident_f, 1.0), writes=["ident_f"])
        P.add("pool", lambda e: e.affine_select(out=ident_f, in_=ident_f, pattern=[[-1, 128]],
                                                compare_op=ALU.is_equal, fill=0.0, base=0, channel_multiplier=1),
              reads=["ident_f"], writes=["ident_f"])
        P.add("pool", lambda e: e.memset(U_f, 1.0), writes=["U_f"])
        P.add("pool", lambda e: e.affine_select(out=U_f, in_=U_f, pattern=[[1, 128]],
                                                compare_op=ALU.is_ge, fill=0.0, base=0, channel_multiplier=-1),
              reads=["U_f"], writes=["U_f"])
        P.add("pool", lambda e: e.memset(SL_f, 1.0), writes=["SL_f"])
        P.add("pool", lambda e: e.affine_select(out=SL_f, in_=SL_f, pattern=[[-1, 128]],
                                                compare_op=ALU.is_gt, fill=0.0, base=0, channel_multiplier=1),
              reads=["SL_f"], writes=["SL_f"])
        P.add("pool", lambda e: e.memset(ones_f, 1.0), writes=["ones_f"])
        P.add("pool", lambda e: e.memset(ones_b, 1.0), writes=["ones_b"])
        P.add("pool", lambda e: e.memset(negh, -0.5), writes=["negh"])
        P.add("pool", lambda e: e.memset(epsln, LN_EPS), writes=["epsln"])
        P.add("dve", lambda e: e.tensor_copy(out=ident_b, in_=ident_f), reads=["ident_f"], writes=["ident_b"])
        P.add("dve", lambda e: e.tensor_copy(out=U_b, in_=U_f), reads=["U_f"], writes=["U_b"])
        P.add("dve", lambda e: e.tensor_copy(out=SL_r, in_=SL_f), reads=["SL_f"], writes=["SL_r"])
        P.add("act", lambda e: e.activation(out=aneg, in_=aneg, func=AF.Exp), reads=["aneg", "cst_all"], writes=["aneg"])
        P.add("dve", lambda e: e.tensor_scalar(out=aneg, in0=aneg, scalar1=-1.0, scalar2=None, op0=ALU.mult),
              reads=["aneg"], writes=["aneg"])
        for q in range(80):
            P.add("dve", lambda e, q=q: e.tensor_scalar(out=convdiag[:, q, :], in0=ident_b,
                                                        scalar1=convw[:, q:q + 1], scalar2=None, op0=ALU.mult),
                  reads=["ident_b", "convw", "cst_all"], writes=["convdiag"])
        bg_load(0)
        P.barrier(lambda e: e.memset(bart, 0.0))

        def layer_norm(base, g_t, b_t, hT_out=None):
            rb = carve_at(base, [8, 512], BF16)
            sqb = carve_at(base + 8192, [8, 512], BF16)
            mean = carve_at(base + 16384, [512], F32)
            rstd = carve_at(base + 18432, [512], F32)
            m2 = carve_at(base + 20480, [512], F32)
            b1, b2 = next_bank(), next_bank()
            for fc in range(8):
                P.add("act", lambda e, fc=fc: e.activation(out=rb[:, fc, :], in_=resid[:, fc, :], func=AF.Copy),
                      reads=["resid%d" % fc], writes=["rb%d" % fc])
                if fc % 2 == 1:
                    P.add("dve", lambda e, fc=fc: e.tensor_tensor(out=sqb[:, fc, :], in0=resid[:, fc, :],
                                                                  in1=resid[:, fc, :], op=ALU.mult),
                          reads=["resid%d" % fc], writes=["sqb%d" % fc])
                else:
                    P.add("act", lambda e, fc=fc: e.activation(out=sqb[:, fc, :], in_=resid[:, fc, :], func=AF.Square),
                          reads=["resid%d" % fc], writes=["sqb%d" % fc])
            for fc in range(8):
                P.add("pe", lambda e, fc=fc: e.matmul(pst[b1][:], lhsT=ones_b, rhs=rb[:, fc, :],
                                                      start=(fc == 0), stop=(fc == 7)),
                      reads=["rb%d" % fc, "ones_b"], writes=[psk(b1)])
            for fc in range(8):
                P.add("pe", lambda e, fc=fc: e.matmul(pst[b2][:], lhsT=ones_b, rhs=sqb[:, fc, :],
                                                      start=(fc == 0), stop=(fc == 7)),
                      reads=["sqb%d" % fc, "ones_b"], writes=[psk(b2)])
            P.add("act", lambda e: e.activation(out=mean, in_=pst[b1][:], func=AF.Copy, scale=1.0 / D),
                  writes=[psk(b1), "ln_mean"])
            P.add("dve", lambda e: e.tensor_tensor(out=m2, in0=mean, in1=mean, op=ALU.mult),
                  reads=["ln_mean"], writes=["ln_m2"])
            P.add("dve", lambda e: e.scalar_tensor_tensor(out=rstd, in0=pst[b2][:], scalar=1.0 / D, in1=m2,
                                                          op0=ALU.mult, op1=ALU.subtract),
                  reads=["ln_m2"], writes=[psk(b2), "ln_rstd"])
            P.add("act", lambda e: e.activation(out=m2, in_=rstd, func=AF.Ln, bias=epsln[:, 0:1]),
                  reads=["ln_rstd"], writes=["ln_m2"])
            P.add("act", lambda e: e.activation(out=rstd, in_=m2, func=AF.Exp, scale=-0.5),
                  reads=["ln_m2"], writes=["ln_rstd"])
            def gain(fc):
                P.add("dve", lambda e, fc=fc: e.tensor_scalar(out=resid[:, fc, :], in0=resid[:, fc, :],
                                                              scalar1=g_t[:, fc:fc + 1], scalar2=b_t[:, fc:fc + 1],
                                                              op0=ALU.mult, op1=ALU.add),
                      reads=["resid%d" % fc], writes=["resid%d" % fc])
            for fc in range(8):
                k = "resid%d" % fc
                P.add("dve", lambda e, fc=fc: e.tensor_tensor(out=resid[:, fc, :], in0=resid[:, fc, :], in1=mean,
                                                              op=ALU.subtract),
                      reads=[k, "ln_mean"], writes=[k])
                P.add("dve", lambda e, fc=fc: e.tensor_tensor(out=resid[:, fc, :], in0=resid[:, fc, :], in1=rstd,
                                                              op=ALU.mult),
                      reads=[k, "ln_rstd"], writes=[k])
                if hT_out is not None:
                    P.add("act", lambda e, fc=fc: e.activation(out=hT_out[:, fc, :], in_=resid[:, fc, :],
                                                               func=AF.Identity, scale=g_t[:, fc:fc + 1],
                                                               bias=b_t[:, fc:fc + 1]),
                          reads=[k], writes=["hTb%d" % fc])
                else:
                    P.add("act", lambda e, fc=fc: e.activation(out=resid[:, fc, :], in_=resid[:, fc, :],
                                                               func=AF.Identity, scale=g_t[:, fc:fc + 1],
                                                               bias=b_t[:, fc:fc + 1]),
                          reads=[k], writes=[k])
            if hT_out is not None:
                for fc in range(8):
                    gain(fc)

        bar = lambda: P.barrier(lambda e: e.memset(bart, 0.0))

        for ti in range(NT):
            j = ti % (SEQ // TT)
            ring_n[0] = 3 if ti == 0 else NRING
            row0 = ti * TT
            def load_x(r0):
                for st in range(4):
                    P.add("sp", lambda e, st=st, r0=r0: e.dma_start(out=x_tok[:, st, :],
                                                             in_=x_d[r0 + st * 128: r0 + (st + 1) * 128, :]),
                          writes=["x_tok%d" % st], stream="xin%d" % st, nobar=True)
            if ti == 0:
                load_x(0)
            def cast_x(all_act):
                for st in range(4):
                    if all_act or st % 2 == 0:
                        P.add("act", lambda e, st=st: e.activation(out=x_bf[:, st, :], in_=x_tok[:, st, :],
                                                                   func=AF.Copy),
                              reads=["x_tok%d" % st], writes=["x_bf%d" % st])
                    else:
                        P.add("dve", lambda e, st=st: e.tensor_copy(out=x_bf[:, st, :], in_=x_tok[:, st, :]),
                              reads=["x_tok%d" % st], writes=["x_bf%d" % st])
            if ti == 0:
                cast_x(False)
            for fc in range(8):
                b = next_bank()
                pb = pst[b][:].bitcast(BF16)
                for st in range(4):
                    P.add("pe", lambda e, pb=pb, st=st, fc=fc: e.transpose(
                        pb[:, st * 128:(st + 1) * 128], x_bf[:, st, fc * 128:(fc + 1) * 128], ident_b),
                        reads=["x_bf%d" % st, "ident_b"], writes=[psk(b)])
                if fc % 2 == 0:
                    P.add("dve", lambda e, pb=pb, fc=fc: e.tensor_copy(out=xTb[:, fc, :], in_=pb[:, 0:512]),
                          writes=[psk(b), "xTb%d" % fc])
                else:
                    P.add("act", lambda e, pb=pb, fc=fc: e.activation(out=xTb[:, fc, :], in_=pb[:, 0:512],
                                                                      func=AF.Copy),
                          writes=[psk(b), "xTb%d" % fc])

            def resid_transposes(fc):
                b = next_bank()
                for st_ in range(4):
                    P.add("pe", lambda e, b=b, st_=st_, fc=fc: e.transpose(
                        pst[b][:, st_ * 128:(st_ + 1) * 128], x_tok[:, st_, fc * 128:(fc + 1) * 128], ident_f),
                        reads=["x_tok%d" % st_, "ident_f"], writes=[psk(b)])
                P.add("act", lambda e, b=b, fc=fc: e.activation(out=resid[:, fc, :], in_=pst[b][:], func=AF.Copy),
                      writes=[psk(b), "resid%d" % fc])
            if j == 0:
                P.add("pool", lambda e: e.memset(xbchalo, 0.0), writes=["xbchalo"])
                P.add("pool", lambda e: e.memset(poolhalo, 0.0), writes=["poolhalo"])
                P.add("pool", lambda e: e.memset(prevf, 0.0), writes=["prevf"])
                P.add("pool", lambda e: e.memset(prevb, 0.0), writes=["prevb"])
            xk = lambda kc: "xTb%d" % kc
            P.add("pool", lambda e: e.tensor_copy(out=poolA[:, :, 0:15], in_=poolhalo),
                  reads=["poolhalo"], writes=["poolA_h"])

            def evac_pool(c, b):
                P.add("act", lambda e, c=c, b=b: e.activation(out=poolA[:, c, 15:527], in_=pst[b][:], func=AF.Copy),
                      writes=[psk(b), "poolA%d" % c])
            gemm(wq_in, "wq_in", D, 0, 512, xTb, xk, evac_pool, kc_outer_first=True)
            P.add("pool", lambda e: e.tensor_copy(out=poolhalo, in_=poolA[:, :, 512:527]),
                  reads=["poolA%d" % c for c in range(4)] + ["poolA_h"], writes=["poolhalo"])
            for g, w in enumerate(WINS):
                src = poolA[:, g, :]
                bufs = [poolB, poolC]
                lv = int(math.log2(w))
                cur = src
                curk = ["poolA%d" % g, "poolA_h"]
                sh = 1
                for l in range(lv):
                    dst = bufs[l % 2]
                    dk = "poolBC%d" % (l % 2)
                    lo = 2 * sh - 1
                    P.add("pool", lambda e, dst=dst, cur=cur, lo=lo, sh=sh: e.tensor_tensor(
                        out=dst[:, lo:527], in0=cur[:, lo:527], in1=cur[:, lo - sh:527 - sh], op=ALU.add),
                        reads=curk, writes=[dk])
                    cur, curk, sh = dst, [dk], sh * 2
                P.add("dve", lambda e, g=g, cur=cur, w=w: e.scalar_tensor_tensor(
                    out=diff[:, g, :], in0=cur[:, 15:527], scalar=1.0 / w, in1=poolA[:, g, 15:527],
                    op0=ALU.mult, op1=ALU.subtract),
                    reads=curk + ["poolA%d" % g], writes=["diff%d" % g])
                if j == 0:
                    for t in range(w - 1):
                        P.add("dve", lambda e, g=g, cur=cur, t=t: e.scalar_tensor_tensor(
                            out=diff[:, g, t:t + 1], in0=cur[:, 15 + t:16 + t], scalar=1.0 / (t + 1),
                            in1=poolA[:, g, 15 + t:16 + t], op0=ALU.mult, op1=ALU.subtract),
                            reads=curk + ["poolA%d" % g, "diff%d" % g], writes=["diff%d" % g])
            for zb in range(3):
                s = ring_load(wq_in, "wq_in", 0, 512 + zb * 512, 512)
                for st in range(4):
                    b = next_bank()
                    for kc in range(8):
                        P.add("pe", lambda e, b=b, s=s, kc=kc, st=st: e.matmul(
                            pst[b][:], lhsT=xTb[:, kc, st * 128:(st + 1) * 128], rhs=ring[s][:, kc, :],
                            start=(kc == 0), stop=(kc == 7)),
                            reads=["ring%d" % s, xk(kc)], writes=[psk(b)])
                    P.add("act", lambda e, b=b, st=st, zb=zb: e.activation(
                        out=sz[:, st, zb * 512:(zb + 1) * 512], in_=pst[b][:], func=AF.Silu),
                        writes=[psk(b), "sz%d_%d" % (st, zb)])
                if ti == 0:
                    bg_step()
            def pool_matmuls():
                for g in range(4):
                    b = next_bank()
                    P.add("pe", lambda e, g=g, b=b: e.matmul(pst[b][:], lhsT=wpool_b[:, g, :], rhs=diff[:, g, :],
                                                             start=True, stop=True),
                          reads=["diff%d" % g, "wpool"], writes=[psk(b)])
                    P.add("act", lambda e, g=g, b=b: e.activation(out=mixedT[:, g, :], in_=pst[b][:], func=AF.Copy,
                                                                  scale=poolsc[:, g:g + 1]),
                          reads=["poolsc"], writes=[psk(b), "mixedT%d" % g])
            pool_matmuls()
            s = ring_load(wq_in, "wq_in", 0, 4608, 24)
            b = next_bank()
            for st in range(4):
                for kc in range(8):
                    P.add("pe", lambda e, b=b, s=s, kc=kc, st=st: e.matmul(
                        pst[b][:, st * 32:st * 32 + 24], lhsT=xTb[:, kc, st * 128:(st + 1) * 128],
                        rhs=ring[s][:, kc, 0:24], start=(kc == 0), stop=(kc == 7)),
                        reads=["ring%d" % s, xk(kc)], writes=[psk(b)])
            P.add("dve", lambda e, b=b: e.tensor_tensor(
                out=dtmp, in0=pst[b][:, 0:128].rearrange("p (a c) -> p a c", a=4)[:, :, 0:24],
                in1=dtb.unsqueeze(1).to_broadcast([128, 4, 24]), op=ALU.add),
                reads=["dtb"], writes=[psk(b), "dtmp"])
            P.add("act", lambda e: e.activation(out=dexp, in_=dtmp, func=AF.Exp), reads=["dtmp"], writes=["dexp"])
            P.add("act", lambda e: e.activation(out=dt_t, in_=dexp, func=AF.Ln, bias=1.0),
                  reads=["dexp"], writes=["dt_t"])
            P.add("dve", lambda e: e.tensor_tensor(out=a_t, in0=dt_t, in1=aneg.unsqueeze(1).to_broadcast([128, 4, 24]),
                                                   op=ALU.mult),
                  reads=["dt_t", "aneg"], writes=["a_t"])
            ectr = [0]

            def evac_xbc(c, b):
                ei = ectr[0] % 2
                ectr[0] += 1
                ek = "ext%d" % ei
                P.add("dve", lambda e, b=b, ei=ei: e.tensor_copy(out=ext[ei][:, 3:515], in_=pst[b][:]),
                      writes=[psk(b), ek])
                P.add("pool", lambda e, c=c, ei=ei: e.tensor_copy(out=ext[ei][:, 0:3], in_=xbchalo[:, c, :]),
                      reads=["xbchalo"], writes=[ek + "h"])
                P.add("pool", lambda e, c=c, ei=ei: e.tensor_copy(out=xbchalo[:, c, :], in_=ext[ei][:, 512:515]),
                      reads=[ek], writes=["xbchalo"])
                b2 = next_bank()
                for k in range(4):
                    P.add("pe", lambda e, b2=b2, c=c, k=k, ei=ei: e.matmul(
                        pst[b2][:], lhsT=convdiag[:, c * 4 + k, :], rhs=ext[ei][:, k:k + 512],
                        start=(k == 0), stop=(k == 3)),
                        reads=[ek, ek + "h", "convdiag"], writes=[psk(b2)])
                P.add("act", lambda e, b2=b2, c=c: e.activation(out=xbcT[:, c, :], in_=pst[b2][:], func=AF.Silu,
                                                                bias=convb[:, c:c + 1]),
                      reads=["convb"], writes=[psk(b2), "xbcT%d" % c])
                if ti == 0 and c % 4 == 3:
                    bg_step()
            gemm(wq_in, "wq_in", D, 2048, 2560, xTb, xk, evac_xbc)
            if ti == 0:
                dump("d_xbcT", xbcT, [128, 20, 512])
                dump("d_sz", sz, [128, 4, 1536])
                dump("d_dt", dt_t, [128, 4, 24])
                dump("d_pool", mixedT[:, 0:4, :], [128, 4, 512])
                dump("d_xT", xTb, [128, 8, 512])
            bar()
            if ti == 0:
                while bg_pos[0] < len(bg_list):
                    bg_step()
                ring_n[0] = NRING
            bank_pool[0] = [0, 1, 2, 3, 4, 5]
            def stageA(st):
                tsl = slice(st * 128, (st + 1) * 128)
                resid_transposes(2 * st)
                resid_transposes(2 * st + 1)
                for half in range(2):
                    b = next_bank()
                    pb = pst[b][:].bitcast(BF16)
                    for q in range(6):
                        P.add("pe", lambda e, pb=pb, q=q, half=half, tsl=tsl: e.transpose(
                            pb[:, q * 128:(q + 1) * 128], xbcT[:, half * 6 + q, tsl], ident_b),
                            reads=["xbcT%d" % (half * 6 + q), "ident_b"], writes=[psk(b)])
                    hs = slice(12 * half, 12 * half + 12)
                    pv = pb[:, 0:768].rearrange("p (h d) -> p h d", d=64)
                    P.add("dve", lambda e, pv=pv, hs=hs, st=st: e.tensor_tensor(
                        out=xdt[:, hs, :], in0=pv, in1=dt_t[:, st, hs].unsqueeze(2).to_broadcast([128, 12, 64]),
                        op=ALU.mult), reads=["dt_t"], writes=[psk(b), "xdt%d" % half])
                    P.add("dve", lambda e, pv=pv, hs=hs: e.tensor_tensor(
                        out=xsD[:, hs, :], in0=pv, in1=dsk[:, hs].unsqueeze(2).to_broadcast([128, 12, 64]),
                        op=ALU.mult), reads=["dsk"], writes=[psk(b), "xsD%d" % half])
                b = next_bank()
                pb = pst[b][:].bitcast(BF16)
                for g in range(4):
                    P.add("pe", lambda e, pb=pb, g=g, tsl=tsl: e.transpose(
                        pb[:, g * 128:(g + 1) * 128], xbcT[:, 12 + g, tsl], ident_b),
                        reads=["xbcT%d" % (12 + g), "ident_b"], writes=[psk(b)])
                P.add("act", lambda e, pb=pb: e.activation(out=Btok, in_=pb[:, 0:512].rearrange("p (g n) -> p g n", g=4),
                                                           func=AF.Copy),
                      writes=[psk(b), "Btok"])
                b = next_bank()
                for g in range(4):
                    P.add("pe", lambda e, b=b, g=g, tsl=tsl: e.matmul(
                        pst[b][:, g * 128:(g + 1) * 128], lhsT=xbcT[:, 12 + g, tsl], rhs=xbcT[:, 16 + g, tsl],
                        start=(g == 0), stop=(g == 3), skip_group_check=True),
                        reads=["xbcT%d" % (12 + g), "xbcT%d" % (16 + g)], writes=[psk(b)])
                P.add("dve", lambda e, b=b: e.tensor_tensor(
                    out=CBm2[st % 2], in0=pst[b][:].rearrange("p (g n) -> p g n", g=4),
                    in1=U_b.unsqueeze(1).to_broadcast([128, 4, 128]), op=ALU.mult),
                    reads=["U_b"], writes=[psk(b), "CBm%d" % (st % 2)])
                b = next_bank()
                P.add("pe", lambda e, b=b, st=st: e.matmul(pst[b][:, 0:24], lhsT=U_f, rhs=a_t[:, st, :],
                                                           start=True, stop=True, skip_group_check=True),
                      reads=["a_t", "U_f"], writes=[psk(b)])
                P.add("pe", lambda e, b=b, st=st: e.matmul(pst[b][:, 32:56], lhsT=ones_f, rhs=a_t[:, st, :],
                                                           start=False, stop=True, skip_group_check=True),
                      reads=["a_t", "ones_f"], writes=[psk(b)])
                P.add("pe", lambda e, b=b, st=st: e.matmul(pst[b][:, 64:88], lhsT=SL_f, rhs=a_t[:, st, :],
                                                           start=False, stop=True, skip_group_check=True),
                      reads=["a_t", "SL_f"], writes=[psk(b)])
                P.add("act", lambda e, b=b: e.activation(out=ea, in_=pst[b][:, 0:24], func=AF.Exp),
                      writes=[psk(b), "ea"])
                P.add("act", lambda e, b=b: e.activation(out=cd, in_=pst[b][:, 32:56], func=AF.Exp),
                      writes=[psk(b), "cd"])
                P.add("act", lambda e, b=b: e.activation(out=dte, in_=pst[b][:, 64:88], func=AF.Exp),
                      writes=[psk(b), "dte"])
            stageA(0)
            for st in range(4):
                tsl = slice(st * 128, (st + 1) * 128)

                def emit_xdec():
                    P.add("pool", lambda e: e.tensor_tensor(out=xdec, in0=xdt,
                                                            in1=dte.unsqueeze(2).to_broadcast([128, 24, 64]),
                                                            op=ALU.mult),
                          reads=["xdt0", "xdt1", "dte"], writes=["xdec"])

                def tri_front(stx, q3):
                    g = q3 // 2
                    ri, mi = (stx * 8 + q3) % 2, (stx * 8 + q3) % 3
                    hs = slice(3 * q3, 3 * q3 + 3)
                    CBm = CBm2[stx % 2]
                    P.add("pool", lambda e, ri=ri, hs=hs, stx=stx: e.tensor_tensor(
                        out=R3[ri], in0=a_t[:, stx, hs].unsqueeze(2).to_broadcast([128, 3, 128]),
                        in1=U_f.unsqueeze(1).to_broadcast([128, 3, 128]), op=ALU.mult),
                        reads=["a_t", "U_f"], writes=["R3_%d" % ri])
                    b = next_bank()
                    P.add("pe", lambda e, b=b, ri=ri: e.matmul(
                        pst[b][:, 0:384], lhsT=SL_r, rhs=R3[ri][:].rearrange("p h l -> p (h l)"),
                        start=True, stop=True),
                        reads=["R3_%d" % ri, "SL_r"], writes=[psk(b)])
                    P.add("act", lambda e, b=b, ri=ri: e.activation(
                        out=L3[ri][:].rearrange("p h l -> p (h l)"), in_=pst[b][:, 0:384], func=AF.Exp),
                        writes=[psk(b), "L3_%d" % ri])
                    P.add("dve", lambda e, ri=ri, mi=mi, g=g, CBm=CBm: e.tensor_tensor(
                        out=M3[mi], in0=L3[ri], in1=CBm[:, g, :].unsqueeze(1).to_broadcast([128, 3, 128]),
                        op=ALU.mult),
                        reads=["L3_%d" % ri, "CBm%d" % (stx % 2)], writes=["M3_%d" % mi])

                def tri_back(q3):
                    g = q3 // 2
                    mi = (st * 8 + q3) % 3
                    yb = 6 + (g % 2)
                    for hh in range(3):
                        h = 3 * q3 + hh
                        hl = h % 6
                        P.add("pe", lambda e, yb=yb, mi=mi, hh=hh, h=h, hl=hl: e.matmul(
                            pst[yb][:, hl * 64:(hl + 1) * 64], lhsT=M3[mi][:, hh, :], rhs=xdt[:, h, :],
                            start=(hl == 0), stop=False, skip_group_check=True),
                            reads=["M3_%d" % mi, "xdt%d" % (h // 12)], writes=[psk(yb)])
                    if q3 % 2 == 1:
                        P.add("pe", lambda e, yb=yb, g=g: e.matmul(
                            pst[yb][:, 0:384], lhsT=ident_b,
                            rhs=xsD[:, 6 * g:6 * g + 6, :].rearrange("p h d -> p (h d)"),
                            start=False, stop=True, skip_group_check=True),
                            reads=["xsD%d" % (g // 2), "ident_b"], writes=[psk(yb)])

                def group_post(g):
                    hs6 = slice(6 * g, 6 * g + 6)
                    ti1 = g % 2
                    yb = 6 + (g % 2)
                    bo = next_bank()
                    P.add("pe", lambda e, bo=bo, g=g, tsl=tsl: e.matmul(
                        pst[bo][:, 0:384], lhsT=xbcT[:, 16 + g, tsl], rhs=prevb[:, g, :], start=True, stop=True),
                        reads=["xbcT%d" % (16 + g), "prevb%d" % g], writes=[psk(bo)])
                    bs = next_bank()
                    P.add("pe", lambda e, bs=bs, g=g, hs6=hs6: e.matmul(
                        pst[bs][:, 0:384], lhsT=Btok[:, g, :], rhs=xdec[:, hs6, :].rearrange("p h d -> p (h d)"),
                        start=True, stop=True),
                        reads=["Btok", "xdec"], writes=[psk(bs)])
                    P.add("dve", lambda e, bo=bo, ti1=ti1, hs6=hs6: e.tensor_tensor(
                        out=t1[ti1], in0=pst[bo][:, 0:384].rearrange("p (h d) -> p h d", d=64),
                        in1=ea[:, hs6].unsqueeze(2).to_broadcast([128, 6, 64]), op=ALU.mult),
                        reads=["ea"], writes=[psk(bo), "t1_%d" % ti1])
                    P.add("dve", lambda e, yb=yb, ti1=ti1: e.tensor_tensor(
                        out=t1[ti1], in0=pst[yb][:, 0:384].rearrange("p (h d) -> p h d", d=64), in1=t1[ti1],
                        op=ALU.add),
                        reads=["t1_%d" % ti1], writes=[psk(yb), "t1_%d" % ti1])
                    P.add("dve", lambda e, g=g, ti1=ti1, st=st: e.tensor_tensor(
                        out=vv[:, g, :], in0=t1[ti1][:].rearrange("p h d -> p (h d)"),
                        in1=sz[:, st, g * 384:(g + 1) * 384], op=ALU.mult),
                        reads=["t1_%d" % ti1] + ["sz%d_%d" % (st, zb) for zb in range(3)], writes=["vv%d" % g])
                    P.add("pool", lambda e, g=g, hs6=hs6, ti1=ti1: e.tensor_tensor(
                        out=t2[ti1], in0=prevf[:, g, :].rearrange("p (h d) -> p h d", d=64),
                        in1=cd[:, hs6].unsqueeze(2).to_broadcast([128, 6, 64]), op=ALU.mult),
                        reads=["prevf%d" % g, "cd"], writes=["t2_%d" % ti1])
                    P.add("dve", lambda e, bs=bs, g=g, ti1=ti1: e.tensor_tensor(
                        out=prevf[:, g, :], in0=pst[bs][:, 0:384], in1=t2[ti1][:].rearrange("p h d -> p (h d)"),
                        op=ALU.add),
                        reads=["t2_%d" % ti1], writes=[psk(bs), "prevf%d" % g])
                    if g > 0:
                        group_act(g - 1)

                def group_act(g):
                    P.add("act", lambda e, g=g: e.activation(out=junk, in_=vv[:, g, :], func=AF.Square,
                                                             accum_out=ss[:, g:g + 1]),
                          reads=["vv%d" % g], writes=["junk", "ss%d" % g])
                    P.add("act", lambda e, g=g: e.activation(out=prevb[:, g, :], in_=prevf[:, g, :], func=AF.Copy),
                          reads=["prevf%d" % g], writes=["prevb%d" % g])

                if st == 0:
                    tri_front(0, 0)
                    tri_front(0, 1)
                for q3 in range(8):
                    if q3 + 2 < 8:
                        tri_front(st, q3 + 2)
                    if q3 == 0:
                        emit_xdec()
                    tri_back(q3)
                    if q3 % 2 == 1:
                        group_post(q3 // 2)
                group_act(3)
                ssk = ["ss%d" % g for g in range(4)]
                P.add("dve", lambda e: e.tensor_scalar(out=ms, in0=ss, scalar1=1.0 / 384, scalar2=RMS_EPS,
                                                       op0=ALU.mult, op1=ALU.add), reads=ssk, writes=["ms"])
                P.add("act", lambda e: e.activation(out=lnms, in_=ms, func=AF.Ln), reads=["ms"], writes=["lnms"])
                P.add("act", lambda e: e.activation(out=rstd4, in_=lnms, func=AF.Exp, scale=-0.5),
                      reads=["lnms"], writes=["rstd4"])
                for g in range(4):
                    P.add("act", lambda e, g=g: e.activation(out=vn[:, g, :], in_=vv[:, g, :], func=AF.Copy,
                                                             scale=rstd4[:, g:g + 1]),
                          reads=["rstd4", "vv%d" % g], writes=["vn%d" % g])
                if st + 1 < 4:
                    stageA(st + 1)
                    tri_front(st + 1, 0)
                    tri_front(st + 1, 1)
                if ti == 0 and st == 0:
                    dump("d_vv0", vv, [128, 4, 384])
                    dump("d_ss", ss, [128, 4])
                    dump("d_rstd4", rstd4, [128, 4])
                    dump("d_ms", ms, [128, 4])
                    dump("d_lnms", lnms, [128, 4])
                    dump("d_vn", vn, [128, 4, 384])
                vnf = vn[:].rearrange("p g d -> p (g d)")
                for half in range(2):
                    b = next_bank()
                    pb = pst[b][:].bitcast(BF16)
                    for q in range(6):
                        c = half * 6 + q
                        P.add("pe", lambda e, pb=pb, q=q, c=c: e.transpose(
                            pb[:, q * 128:(q + 1) * 128], vnf[:, c * 128:(c + 1) * 128], ident_b),
                            reads=["vn%d" % (c // 3), "ident_b"], writes=[psk(b)])
                    P.add("dve", lambda e, pb=pb, half=half, tsl=tsl: e.tensor_tensor(
                        out=mixedT[:, 4 + 6 * half:10 + 6 * half, tsl],
                        in0=pb[:, 0:768].rearrange("p (c t) -> p c t", c=6),
                        in1=normw[:, 6 * half:6 * half + 6].unsqueeze(2).to_broadcast([128, 6, 128]), op=ALU.mult),
                        reads=["normw"], writes=[psk(b)] + ["mixedT%d" % (4 + 6 * half + q) for q in range(6)])
            if ti == 0:
                dump("d_mixedT", mixedT, [128, 16, 512])
                dump("d_vv", vv, [128, 4, 384])
                dump("d_prevf", prevf, [128, 4, 384])
            if ti + 1 < NT:
                load_x((ti + 1) * TT)
            bank_pool[0] = list(range(8))
            def evac_res(c, b):
                P.add("dve", lambda e, c=c, b=b: e.scalar_tensor_tensor(
                    out=resid[:, c, :], in0=resid[:, c, :], scalar=ALPHA, in1=pst[b][:],
                    op0=ALU.mult, op1=ALU.add), reads=["resid%d" % c], writes=[psk(b), "resid%d" % c])
            gemm(wq_out, "wq_out", 2048, 0, D, mixedT, lambda kc: "mixedT%d" % kc, evac_res)
            layer_norm(LN3, ln1g, ln1b, hT_out=hTb)
            if ti == 0:
                dump("d_h", resid, [128, 8, 512])
            rctr = [0]

            def evac_ff1(c, b):
                ri = rctr[0] % 2
                rctr[0] += 1
                P.add("act", lambda e, b=b, ri=ri: e.activation(out=rl[ri], in_=pst[b][:], func=AF.Relu),
                      writes=[psk(b), "rl%d" % ri])
                P.add("dve", lambda e, c=c, ri=ri: e.tensor_tensor(out=hid[:, c, :], in0=rl[ri], in1=rl[ri],
                                                                   op=ALU.mult),
                      reads=["rl%d" % ri], writes=["hid%d" % c])
            gemm(wq_ff1, "wq_ff1", D, 0, DFF, hTb, lambda kc: "hTb%d" % kc, evac_ff1, kc_outer_first=True)
            gemm(wq_ff2, "wq_ff2", DFF, 0, D, hid, lambda kc: "hid%d" % kc, evac_res)
            if ti + 1 < NT:
                cast_x(True)
            layer_norm(LN4, ln2g, ln2b)
            for st in range(4):
                oi = st % 2
                for hf in range(2):
                    b = next_bank()
                    for q in range(4):
                        fc = hf * 4 + q
                        P.add("pe", lambda e, b=b, q=q, fc=fc, st=st: e.transpose(
                            pst[b][:, q * 128:(q + 1) * 128], resid[:, fc, st * 128:(st + 1) * 128], ident_f),
                            reads=["resid%d" % fc, "ident_f"], writes=[psk(b)])
                    if hf == 0:
                        P.add("act", lambda e, b=b, oi=oi: e.activation(out=ostage[oi][:, 0:512], in_=pst[b][:],
                                                                        func=AF.Copy),
                              writes=[psk(b), "ost%d_0" % oi])
                    else:
                        P.add("dve", lambda e, b=b, oi=oi: e.tensor_copy(out=ostage[oi][:, 512:1024], in_=pst[b][:]),
                              writes=[psk(b), "ost%d_1" % oi])
                P.add("act", lambda e, oi=oi, st=st, row0=row0: e.dma_start(
                    out=out_d[row0 + st * 128: row0 + (st + 1) * 128, :], in_=ostage[oi]),
                    reads=["ost%d_0" % oi, "ost%d_1" % oi], stream="outst%d" % oi)
            bar()
        P.emit(ctx)
        build_nc.stats = P.stats
    return nc


_NC = None


def kernel(x, w_in, w_pool, pool_scale, conv_w, conv_b, dt_bias, a_log, d_skip, ssd_norm_w, w_out,
           ln1_g, ln1_b, w_ff1, w_ff2, ln2_g, ln2_b):
    global _NC
    if _NC is None:
        _NC = build_nc()
    nc = _NC
    f = lambda a: np.ascontiguousarray(np.asarray(a, dtype=np.float32))
    x = f(x).reshape(16 * SEQ, D)
    rep = lambda v: f(np.broadcast_to(f(v).reshape(1, -1), (128, f(v).size)))
    colm = lambda v, n: f(f(v).reshape(n, 128).T)
    shared = {
        "w_in": f(w_in[0]), "w_out": f(w_out[0]), "w_ff1": f(w_ff1[0]), "w_ff2": f(w_ff2[0]),
        "w_pool_l": f(np.transpose(f(w_pool[0]), (1, 0, 2))),
        "conv_w_l": f(np.transpose(f(conv_w[0]).reshape(4, 20, 128), (2, 1, 0)).reshape(128, 80)),
        "conv_b_l": colm(conv_b[0], 20),
        "pool_scale_l": colm(pool_scale[0], 4),
        "ssd_norm_w_l": colm(ssd_norm_w[0], 12),
        "ln1_g_l": colm(ln1_g[0], 8), "ln1_b_l": colm(ln1_b[0], 8),
        "ln2_g_l": colm(ln2_g[0], 8), "ln2_b_l": colm(ln2_b[0], 8),
        "dt_bias_r": rep(dt_bias[0]), "a_log_r": rep(a_log[0]), "d_skip_r": rep(d_skip[0]),
    }
    in_maps = []
    for c in range(NCORES):
        m = dict(shared)
        m["x"] = np.ascontiguousarray(x[c * TOK:(c + 1) * TOK])
        in_maps.append(m)
    res = run_bass_kernel_spmd(nc, in_maps, core_ids=list(range(NCORES)))
    out = np.concatenate([np.asarray(r["out"], dtype=np.float32) for r in res.results], axis=0)
    return out.reshape(16, SEQ, D)
```
